# Optimizing a Trainium2 kernel written in Bass

```python
import jax
import jax.numpy as jnp
from jax import lax
import numpy as np

D_MODEL = 1024
BATCH = 2
SEQ = 8192
DEPTH = 1
DEC_BATCH = 128
DEC_SEQ = 1
PAST_LEN = 16384
PAGE_SIZE = 128

N_META = 16
WINDOW = 128
ATT_HEAD_DIM = 64
ATT_Q_HEADS = 8
ATT_KV_HEADS = 2
ATT_GROUP = ATT_Q_HEADS // ATT_KV_HEADS
D_ATT = ATT_Q_HEADS * ATT_HEAD_DIM
D_KV = ATT_KV_HEADS * ATT_HEAD_DIM
HG_HEADS = 4
HG_DK = 128
HG_DV = 128
D_HG = HG_HEADS * HG_DK
HG_CHUNK = 64
D_FF = 4 * D_MODEL
EPS = 1e-6
IN_SIZES = (D_ATT, D_KV, D_KV, D_HG, D_HG, D_HG, D_HG, D_MODEL, D_MODEL)
D_IN = D_ATT + 2 * D_KV + 4 * D_HG + 2 * D_MODEL

kernel_name = 'hybrid_swa_sink_hgrn2_decoder_step'


def rmsnorm(x, g):
    xf = x.astype(jnp.float32)
    y = xf * lax.rsqrt(jnp.mean(jnp.square(xf), axis=-1, keepdims=True) + EPS)
    return (y * g.astype(jnp.float32)).astype(x.dtype)


def split_projection(z):
    lead = z.shape[:-1]
    parts, off = [], 0
    for n in IN_SIZES:
        parts.append(z[..., off:off + n])
        off += n
    qa, ka, va, qh, fh, ih, gh, ga, gb = parts
    return (qa.reshape(*lead, ATT_Q_HEADS, ATT_HEAD_DIM),
            ka.reshape(*lead, ATT_KV_HEADS, ATT_HEAD_DIM),
            va.reshape(*lead, ATT_KV_HEADS, ATT_HEAD_DIM),
            qh.reshape(*lead, HG_HEADS, HG_DK),
            fh.reshape(*lead, HG_HEADS, HG_DK),
            ih.reshape(*lead, HG_HEADS, HG_DV),
            gh.reshape(*lead, HG_HEADS, HG_DV),
            ga, gb)


def sink_probs(s, mask, sinks):
    sk = sinks.astype(jnp.float32).reshape(ATT_KV_HEADS, ATT_GROUP, 1, 1)
    s = jnp.where(mask, s, -jnp.inf)
    m = jnp.maximum(jnp.max(s, axis=-1, keepdims=True), sk)
    p = jnp.exp(s - m)
    return p / (jnp.sum(p, axis=-1, keepdims=True) + jnp.exp(sk - m))


def swa_prompt(q, k, v, sinks):
    B, T = q.shape[:2]
    pad = WINDOW - N_META
    nb = (T + pad) // WINDOW
    pw = ((0, 0), (pad, 0), (0, 0), (0, 0))
    qb = jnp.pad(q, pw).reshape(B, nb, WINDOW, ATT_KV_HEADS, ATT_GROUP, ATT_HEAD_DIM)
    kb = jnp.pad(k, pw).reshape(B, nb, WINDOW, ATT_KV_HEADS, ATT_HEAD_DIM)
    vb = jnp.pad(v, pw).reshape(B, nb, WINDOW, ATT_KV_HEADS, ATT_HEAD_DIM)

    def band_keys(a, meta):
        prev = jnp.pad(a, ((0, 0), (1, 0), (0, 0), (0, 0), (0, 0)))[:, :-1]
        meta_b = jnp.broadcast_to(meta[:, None], (B, nb, N_META, ATT_KV_HEADS, ATT_HEAD_DIM))
        return jnp.concatenate([meta_b, prev, a], axis=2)

    kk = band_keys(kb, k[:, :N_META])
    vv = band_keys(vb, v[:, :N_META])
    s = jnp.einsum('bnqhgd,bnkhd->bnhgqk', qb, kk,
                   preferred_element_type=jnp.float32) * (ATT_HEAD_DIM ** -0.5)
    blk = jnp.arange(nb)[:, None, None]
    qpos = blk * WINDOW + jnp.arange(WINDOW)[None, :, None]
    kpos = (blk - 1) * WINDOW + jnp.arange(2 * WINDOW)[None, None, :]
    rel = qpos - kpos
    m_reg = (rel >= 0) & (rel < WINDOW) & (kpos >= pad + N_META)
    m_meta = qpos >= pad + jnp.arange(N_META)[None, None, :]
    mask = jnp.concatenate([m_meta, m_reg], axis=-1)
    p = sink_probs(s, mask[None, :, None, None], sinks)
    o = jnp.einsum('bnhgqk,bnkhd->bnqhgd', p, vv.astype(jnp.float32))
    return o.reshape(B, nb * WINDOW, D_ATT)[:, pad:]


def swa_sample(q, k, v, k_win, v_win, k_meta, v_meta, sinks):
    Bd, S = q.shape[:2]
    R = k_win.shape[1]
    kw = jnp.concatenate([k_win.astype(k.dtype), k], axis=1)
    vw = jnp.concatenate([v_win.astype(v.dtype), v], axis=1)
    kk = jnp.concatenate([k_meta.astype(k.dtype), kw], axis=1)
    vv = jnp.concatenate([v_meta.astype(v.dtype), vw], axis=1)
    qg = q.reshape(Bd, S, ATT_KV_HEADS, ATT_GROUP, ATT_HEAD_DIM)
    s = jnp.einsum('bqhgd,bkhd->bhgqk', qg, kk,
                   preferred_element_type=jnp.float32) * (ATT_HEAD_DIM ** -0.5)
    qpos = PAST_LEN + jnp.arange(S)[:, None]
    kpos = PAST_LEN - R + jnp.arange(R + S)[None, :]
    rel = qpos - kpos
    m_reg = (rel >= 0) & (rel < WINDOW) & (kpos >= N_META)
    mask = jnp.concatenate([jnp.ones((S, N_META), dtype=bool), m_reg], axis=-1)
    p = sink_probs(s, mask, sinks)
    o = jnp.einsum('bhgqk,bkhd->bqhgd', p, vv.astype(jnp.float32)).reshape(Bd, S, D_ATT)
    return o, kw[:, S:], vw[:, S:]


def hgrn2_gates(qh, fh, ih, lb):
    q = qh.astype(jnp.float32) * (HG_DK ** -0.5)
    f = lb + (1.0 - lb) * jax.nn.sigmoid(fh.astype(jnp.float32))
    return q, f, ih.astype(jnp.float32)


def hgrn2_prompt(q, f, i_val):
    B, T = q.shape[:2]
    pad = (-N_META) % HG_CHUNK
    k = 1.0 - f
    logf = jnp.log(f)
    pw = ((0, 0), (pad, 0), (0, 0), (0, 0))
    L = T + pad
    nc = L // HG_CHUNK

    def rs(a):
        return jnp.pad(a, pw).reshape(B, nc, HG_CHUNK, HG_HEADS, a.shape[-1])

    q, k, logf, i_val = rs(q), rs(k), rs(logf), rs(i_val)
    b = jnp.cumsum(logf, axis=2)
    b_last = b[:, :, -1:]
    qe = q * jnp.exp(b)
    ke = k * jnp.exp(-b)
    kd = k * jnp.exp(b_last - b)
    causal = jnp.tril(jnp.ones((HG_CHUNK, HG_CHUNK), dtype=bool))
    A = jnp.where(causal, jnp.einsum('bnthd,bnshd->bnhts', qe, ke), 0.0)
    o_intra = jnp.einsum('bnhts,bnshv->bnthv', A, i_val)
    U = jnp.einsum('bnshd,bnshv->bnhdv', kd, i_val)
    decay = jnp.exp(b_last[:, :, 0])

    def body(S, xs):
        dec, u = xs
        return dec[..., None] * S + u, S

    S0 = jnp.zeros((B, HG_HEADS, HG_DK, HG_DV), jnp.float32)
    S_fin, S_prev = lax.scan(body, S0, (jnp.swapaxes(decay, 0, 1), jnp.swapaxes(U, 0, 1)))
    S_prev = jnp.swapaxes(S_prev, 0, 1)
    o_inter = jnp.einsum('bnthd,bnhdv->bnthv', qe, S_prev)
    o = (o_intra + o_inter).reshape(B, L, HG_HEADS, HG_DV)[:, pad:]
    return o, S_fin


def hgrn2_sample(q, f, i_val, S0):
    def step(S, xs):
        qt, ft, it = xs
        S = ft[..., None] * S + (1.0 - ft)[..., None] * it[:, :, None, :]
        return S, jnp.einsum('bhd,bhdv->bhv', qt, S)

    S, o = lax.scan(step, S0.astype(jnp.float32),
                    (jnp.swapaxes(q, 0, 1), jnp.swapaxes(f, 0, 1), jnp.swapaxes(i_val, 0, 1)))
    return jnp.swapaxes(o, 0, 1), S


def hgrn2_out(o, g, norm_w):
    o = o * lax.rsqrt(jnp.mean(jnp.square(o), axis=-1, keepdims=True) + EPS) * norm_w.astype(jnp.float32)
    o = o * jax.nn.silu(g.astype(jnp.float32))
    return o.reshape(*o.shape[:2], HG_HEADS * HG_DV)


def merge_branches(att, hg, ga, gb, w_att_out, w_hg_out, w_o):
    ya = att.astype(w_att_out.dtype) @ w_att_out
    yb = hg.astype(w_hg_out.dtype) @ w_hg_out
    return (jax.nn.sigmoid(ga) * ya + jax.nn.sigmoid(gb) * yb) @ w_o


def ffn_residual(h, ln, w_up, w_down):
    u = rmsnorm(h, ln) @ w_up
    return h + jnp.square(jax.nn.relu(u)) @ w_down


def setup_inputs(seed: int = 0) -> dict:
    key = jax.random.key(seed)
    ks = jax.random.split(key, 20)

    def nrm(k, shape, s):
        return jax.random.normal(k, shape, jnp.float32) * s

    rows = min(WINDOW, PAST_LEN)
    return {
        'x_prompt': nrm(ks[0], (BATCH, SEQ, D_MODEL), 1.0),
        'x_sample': nrm(ks[1], (DEC_BATCH, DEC_SEQ, D_MODEL), 1.0),
        'cache_k': nrm(ks[2], (DEPTH, DEC_BATCH, rows, ATT_KV_HEADS, ATT_HEAD_DIM), 1.0),
        'cache_v': nrm(ks[3], (DEPTH, DEC_BATCH, rows, ATT_KV_HEADS, ATT_HEAD_DIM), 1.0),
        'cache_meta_k': nrm(ks[4], (DEPTH, DEC_BATCH, N_META, ATT_KV_HEADS, ATT_HEAD_DIM), 1.0),
        'cache_meta_v': nrm(ks[5], (DEPTH, DEC_BATCH, N_META, ATT_KV_HEADS, ATT_HEAD_DIM), 1.0),
        'state_hgrn': nrm(ks[6], (DEPTH, DEC_BATCH, HG_HEADS, HG_DK, HG_DV), 0.3),
        'meta': nrm(ks[7], (N_META, D_MODEL), 1.0),
        'w_in': nrm(ks[8], (DEPTH, D_MODEL, D_IN), D_MODEL ** -0.5),
        'sinks': nrm(ks[9], (DEPTH, ATT_Q_HEADS), 0.5),
        'lb_param': 1.0 + nrm(ks[10], (DEPTH + 1, D_HG), 0.5),
        'hg_norm': 1.0 + nrm(ks[11], (DEPTH, HG_DV), 0.05),
        'w_att_out': nrm(ks[12], (DEPTH, D_ATT, D_MODEL), D_ATT ** -0.5),
        'w_hg_out': nrm(ks[13], (DEPTH, D_HG, D_MODEL), D_HG ** -0.5),
        'w_o': nrm(ks[14], (DEPTH, D_MODEL, D_MODEL), D_MODEL ** -0.5),
        'ln_mix': 1.0 + nrm(ks[15], (DEPTH, D_MODEL), 0.05),
        'ln_ffn': 1.0 + nrm(ks[16], (DEPTH, D_MODEL), 0.05),
        'w_up': nrm(ks[17], (DEPTH, D_MODEL, D_FF), D_MODEL ** -0.5),
        'w_down': nrm(ks[18], (DEPTH, D_FF, D_MODEL), D_FF ** -0.5),
        'ln_f': 1.0 + nrm(ks[19], (D_MODEL,), 0.05),
    }


def reference(x_prompt, x_sample, cache_k, cache_v, cache_meta_k, cache_meta_v, state_hgrn,
              meta, w_in, sinks, lb_param, hg_norm, w_att_out, w_hg_out, w_o,
              ln_mix, ln_ffn, w_up, w_down, ln_f):
    lb_all = jnp.cumsum(jax.nn.softmax(lb_param.astype(jnp.float32), axis=0), axis=0)
    B = x_prompt.shape[0]
    meta_b = jnp.broadcast_to(meta.astype(x_prompt.dtype)[None], (B, N_META, D_MODEL))
    hp = jnp.concatenate([meta_b, x_prompt], axis=1)
    hs = x_sample
    kp_l, vp_l, mkp_l, mvp_l, sp_l, ks_l, vs_l, ss_l = [], [], [], [], [], [], [], []
    for l in range(DEPTH):
        lb = lb_all[l].reshape(HG_HEADS, HG_DK)
        qa, ka, va, qh, fh, ih, gh, ga, gb = split_projection(rmsnorm(hp, ln_mix[l]) @ w_in[l])
        att = swa_prompt(qa, ka, va, sinks[l])
        q, f, i_val = hgrn2_gates(qh, fh, ih, lb)
        o, s_fin = hgrn2_prompt(q, f, i_val)
        hp = hp + merge_branches(att, hgrn2_out(o, gh, hg_norm[l]), ga, gb,
                                 w_att_out[l], w_hg_out[l], w_o[l])
        hp = ffn_residual(hp, ln_ffn[l], w_up[l], w_down[l])
        kp_l.append(ka[:, -WINDOW:])
        vp_l.append(va[:, -WINDOW:])
        mkp_l.append(ka[:, :N_META])
        mvp_l.append(va[:, :N_META])
        sp_l.append(s_fin.astype(state_hgrn.dtype))
        qa, ka, va, qh, fh, ih, gh, ga, gb = split_projection(rmsnorm(hs, ln_mix[l]) @ w_in[l])
        att, k_new, v_new = swa_sample(qa, ka, va, cache_k[l], cache_v[l],
                                       cache_meta_k[l], cache_meta_v[l], sinks[l])
        q, f, i_val = hgrn2_gates(qh, fh, ih, lb)
        o, s_new = hgrn2_sample(q, f, i_val, state_hgrn[l])
        hs = hs + merge_branches(att, hgrn2_out(o, gh, hg_norm[l]), ga, gb,
                                 w_att_out[l], w_hg_out[l], w_o[l])
        hs = ffn_residual(hs, ln_ffn[l], w_up[l], w_down[l])
        ks_l.append(k_new)
        vs_l.append(v_new)
        ss_l.append(s_new.astype(state_hgrn.dtype))
    y_prompt = rmsnorm(hp, ln_f)[:, N_META:]
    y_sample = rmsnorm(hs, ln_f)
    return (y_prompt, y_sample, jnp.stack(kp_l), jnp.stack(vp_l), jnp.stack(mkp_l), jnp.stack(mvp_l),
            jnp.stack(sp_l), jnp.stack(ks_l), jnp.stack(vs_l), jnp.stack(ss_l))
```

```python
import numpy as np
from contextlib import ExitStack
import concourse.bass as bass
import concourse.mybir as mybir
from concourse.bass_utils import run_bass_kernel_spmd

F32 = mybir.dt.float32
BF16 = mybir.dt.bfloat16
AF = mybir.ActivationFunctionType
ALU = mybir.AluOpType

NCORES = 8
D = 1024
SEQ = 8192
NSEG = 4
TOK = SEQ // NSEG
NT = TOK // 128
NPRE = 49
NS = 16
DIN = 4864
DFF = 4096
EPS = 1e-6
OQ, OK_, OV, OQH, OFH, OIH, OGH, OGA, OGB = 0, 512, 640, 768, 1280, 1792, 2304, 2816, 3840


class Sched:
    EPOCH = 4000
    ROT = 8

    def __init__(self):
        self.ops = []
        self.lw = {}
        self.rd = {}
        self.fdeps = []
        self.limit = None
        self.rec = None

    def fence(self, dma_ops=()):
        last = {}
        for i, o in enumerate(self.ops):
            if not o['dma']:
                last[o['eng']] = i
        self.fdeps = list(last.values()) + list(dma_ops)

    def add(self, eng, fn, reads=(), writes=(), dma=False, extra=()):
        if self.limit is not None and len(self.ops) >= self.limit:
            return 0
        if self.rec is not None:
            self.rec.append((eng, fn, tuple(reads), tuple(writes), dma, tuple(extra)))
            return -1
        deps = {}

        def dep(i):
            o = self.ops[i]
            if o['dma']:
                deps[('d', i)] = i
            else:
                k = ('e', o['eng'])
                if k not in deps or deps[k] < i:
                    deps[k] = i
        for k in reads:
            if k in self.lw:
                dep(self.lw[k])
            if len(k) == 2 and k[0] in 'BT' and k[1].isdigit():
                for rk, i in self.rd.get(k, {}).items():
                    if rk != eng:
                        dep(i)
        for k in writes:
            if k in self.lw:
                dep(self.lw[k])
            for i in self.rd.get(k, {}).values():
                dep(i)
        for i in extra:
            dep(i)
        for i in self.fdeps:
            dep(i)
        idx = len(self.ops)
        self.ops.append(dict(eng=eng, fn=fn, deps=sorted(deps.values()), dma=dma, ms=False))
        rk = ('d', idx) if dma else eng
        for k in reads:
            self.rd.setdefault(k, {})[rk] = idx
        for k in writes:
            self.lw[k] = idx
            self.rd[k] = {}
        return idx

    def readers(self, key):
        r = list(self.rd.get(key, {}).values())
        if key in self.lw:
            r.append(self.lw[key])
        return r

    def prepare(self, nc, stack):
        ops = self.ops
        engs = ['pe', 'act', 'dve', 'pool', 'sp']
        for o in ops:
            for d in o['deps']:
                od = ops[d]
                if od['dma']:
                    continue
                if od['eng'] == 'pe' and o['eng'] == 'pe' and not o['dma']:
                    continue
                od['ms'] = True
        cnt = {e: 0 for e in engs}
        dcnt = {e: 0 for e in engs}
        for o in ops:
            if o['dma']:
                o['dn'] = dcnt[o['eng']]
                dcnt[o['eng']] += 1
            elif o['ms']:
                cnt[o['eng']] += 1
                o['msn'] = cnt[o['eng']]
        sems = {}
        for e in engs:
            n = (cnt[e] + self.EPOCH - 1) // self.EPOCH
            sems[e] = [stack.enter_context(nc.semaphore(f"s_{e}_{i}")) for i in range(max(n, 1))]
        dsems = {}
        for e in engs:
            if dcnt[e]:
                dsems[e] = [stack.enter_context(nc.semaphore(f"d_{e}_{i}")) for i in range(self.ROT)]
        known = {e: {f: 0 for f in engs} for e in engs}
        knownd = {e: {} for e in engs}
        streams = {e: [] for e in engs}

        def ms_wait(o_eng, waits, od):
            m = od['msn']
            if known[o_eng][od['eng']] >= m:
                return
            known[o_eng][od['eng']] = m
            waits.append((sems[od['eng']][(m - 1) // self.EPOCH], (m - 1) % self.EPOCH + 1))

        def d_wait(o_eng, waits, q, n):
            s = dsems[q][n % self.ROT]
            v = 16 * (n // self.ROT + 1)
            if knownd[o_eng].get((q, n % self.ROT), 0) >= v:
                return
            knownd[o_eng][(q, n % self.ROT)] = v
            waits.append((s, v))

        for o in ops:
            e = o['eng']
            waits = []
            for d in o['deps']:
                od = ops[d]
                if od['dma']:
                    d_wait(e, waits, od['eng'], od['dn'])
                else:
                    if od['eng'] == 'pe' and e == 'pe' and not o['dma']:
                        continue
                    ms_wait(e, waits, od)
            inc = None
            if o['dma']:
                n = o['dn']
                if n >= self.ROT:
                    d_wait(e, waits, e, n - self.ROT)
                inc = (dsems[e][n % self.ROT], 16)
            elif o['ms']:
                m = o['msn']
                inc = (sems[e][(m - 1) // self.EPOCH], 1)
            streams[e].append((waits, o['fn'], inc))
        fin = []
        for q in engs:
            for n in range(max(0, dcnt[q] - self.ROT), dcnt[q]):
                fin.append((dsems[q][n % self.ROT], 16 * (n // self.ROT + 1)))

        self.streams, self.fin = streams, fin
        return {e: len(streams[e]) for e in engs}

    def emit(self, block):
        streams, fin = self.streams, self.fin

        def run(eng_obj, name):
            for waits, fn, inc in streams[name]:
                for s, v in waits:
                    eng_obj.wait_ge(s, v)
                ins = fn(eng_obj)
                if inc is not None:
                    ins.then_inc(inc[0], inc[1])
            if name == 'sp':
                for s, v in fin:
                    eng_obj.wait_ge(s, v)

        @block.tensor
        def _(e):
            run(e, 'pe')

        @block.scalar
        def _(e):
            run(e, 'act')

        @block.vector
        def _(e):
            run(e, 'dve')

        @block.gpsimd
        def _(e):
            run(e, 'pool')

        @block.sync
        def _(e):
            run(e, 'sp')


def build_program(npre=NPRE, nt=NT, do_sample=True, do_ffn=True, stop=0, limit=None):
    nc = bass.Bass("TRN2", target_bir_lowering=False)
    S = Sched()
    S.limit = limit
    st = ExitStack()

    def finish():
        counts = S.prepare(nc, st)
        blk = st.enter_context(nc.Block())
        S.emit(blk)
        st.close()
        return nc, counts

    def din(name, shape):
        return nc.dram_tensor(name, list(shape), F32, kind="ExternalInput").ap()

    def dout(name, shape):
        return nc.dram_tensor(name, list(shape), F32, kind="ExternalOutput").ap()

    xp = din("xp", [NPRE * 128, D])
    xmeta = din("xmeta", [16, D])
    xhalo = din("xhalo", [128, D])
    xmain = din("xmain", [TOK, D])
    xs = din("xs", [NS, D])
    w_in = din("w_in", [D, DIN])
    w_ao = din("w_ao", [512, D])
    w_ho = din("w_ho", [512, D])
    w_o = din("w_o", [D, D])
    w_up = din("w_up", [D, DFF])
    w_dn = din("w_dn", [DFF, D])
    gains = din("gains", [128, 3, 8])
    gains_f = din("gains_f", [128, D])
    gains_m = din("gains_m", [128, D])
    gains_b = din("gains_b", [128, D])
    lbp = din("lbp", [128, 8])
    hgn = din("hgn", [128, 1])
    snk = din("snk", [128, 4])
    flag = din("flag", [128, 1])
    ck = din("ck", [NS, 128, 128])
    cv = din("cv", [NS, 128, 128])
    cmk = din("cmk", [NS, 16, 128])
    cmv = din("cmv", [NS, 16, 128])
    sth = din("sth", [NS, 4, 128, 128])

    y = dout("y", [TOK, D])
    ys = dout("ys", [NS, D])
    kp = dout("kp", [128, 128])
    vp = dout("vp", [128, 128])
    mk = dout("mk", [16, 128])
    mv = dout("mv", [16, 128])
    spo = dout("spo", [4, 128, 128])
    ksn = dout("ksn", [NS, 128, 128])
    vsn = dout("vsn", [NS, 128, 128])
    ssn = dout("ssn", [NS, 4, 128, 128])
    h1s = nc.dram_tensor("h1s", [TOK, D], F32, kind="Internal").ap()
    wupb = nc.dram_tensor("wupb", [D, DFF], BF16, kind="Internal").ap()
    wdnb = nc.dram_tensor("wdnb", [DFF, D], BF16, kind="Internal").ap()

    def sb(name, shape, dt=F32):
        return st.enter_context(nc.sbuf_tensor(name, list(shape), dt))

    def pst(name, shape, dt=F32):
        return st.enter_context(nc.psum_tensor(name, list(shape), dt))

    arena = sb("arena", [128, 65536], BF16)
    WIN = arena[:, 0:8 * DIN].rearrange("p (k n) -> p k n", k=8)
    o1 = 8 * DIN
    WAO = arena[:, o1:o1 + 4096].rearrange("p (k n) -> p k n", k=4)
    WHO = arena[:, o1 + 4096:o1 + 8192].rearrange("p (k n) -> p k n", k=4)
    WO = arena[:, o1 + 8192:o1 + 16384].rearrange("p (k n) -> p k n", k=8)
    WUP = arena[:, 0:32768].rearrange("p (k n) -> p k n", k=8)
    t0_ = o1 + 16384

    def tailf(off_b, shape):
        nel = int(np.prod(shape[1:]))
        v = arena[:, t0_ + off_b // 2:t0_ + off_b // 2 + nel * 2].bitcast(F32)
        if len(shape) == 3:
            v = v.rearrange("p (a b) -> p a b", a=shape[1])
        return v
    Ssm2 = [tailf(0, [128, 8, 128]), tailf(4096, [128, 8, 128])]
    kst = tailf(8192, [128, 8, 128])
    vst = tailf(12288, [128, 8, 128])
    mst = tailf(16384, [128, 2, 128])
    iblk_b = tailf(17408, [128, 512])[0:NS]
    WDN = arena[:, 32768:65536].rearrange("p (k n) -> p k n", k=32)

    gT = sb("gT", [128, 3, 8])
    gbt = sb("gbt", [128, D])
    ident_b = sb("ident_b", [128, 128], BF16)
    ident_f = sb("ident_f", [128, 128])
    ones_b = sb("ones_b", [128, 512], BF16)
    ones_f = sb("ones_f", [128, 128])
    zeros_f = sb("zeros_f", [128, 64])
    mhalf = sb("mhalf", [128, 512], BF16)
    oz = [sb(f"oz{h}", [128, 128], BF16) for h in range(2)]
    m_cur = sb("m_cur", [128, 512], BF16)
    m_prev = sb("m_prev", [128, 512], BF16)
    m_prev0 = sb("m_prev0", [128, 512], BF16)
    m_hg = sb("m_hg", [128, 512], BF16)
    prm = sb("prm", [128, 32])
    lbp_t = sb("lbp_t", [128, 8])
    snk_t = sb("snk_t", [128, 4])
    esink = sb("esink", [128, 512])
    xt = [sb(f"xt{i}", [128, D]) for i in range(2)]
    xn = sb("xn", [128, D], BF16)
    ss = sb("ss", [128, 4])
    xnT = [sb(f"xnT{i}", [128, 8, 128], BF16) for i in range(2)]
    th = sb("th", [128, 512])
    fk = sb("fk", [128, 2, 512])
    sgh = sb("sgh", [128, 512])
    osq = sb("osq", [128, 512], BF16)
    mse = sb("mse", [128, 512])
    hgT = sb("hgT", [128, 512], BF16)
    qT = sb("qT", [128, 512], BF16)
    attT = sb("attT", [128, 512], BF16)
    rden = sb("rden", [128, 512])
    kvo = sb("kvo", [128, 256])
    tha = sb("tha", [128, 1024], BF16)
    thb = sb("thb", [128, 1024], BF16)
    t1m = sb("t1m", [128, 512])
    t2m = sb("t2m", [128, 512])
    mixT = sb("mixT", [128, 1024], BF16)
    OVL_B = 21760
    ovl = sb("ovl", [128, OVL_B // 2], BF16)

    class Carver:
        def __init__(self):
            self.off = 0

        def take(self, shape, dt):
            esz = 4 if dt == F32 else 2
            nel = int(np.prod(shape[1:]))
            nb = nel * esz
            assert self.off % 4 == 0
            assert self.off + nb <= OVL_B, (self.off, nb)
            if dt == F32:
                v = ovl[:, self.off // 2:(self.off + nb) // 2].bitcast(F32)
            else:
                v = ovl[:, self.off // 2:(self.off + nb) // 2]
            self.off += nb
            if len(shape) == 3:
                v = v.rearrange("p (a b) -> p a b", a=shape[1])
            return v[0:shape[0]]

    cvA = Carver()
    cp = cvA.take([128, 512], F32)
    rc = cvA.take([128, 512], F32)
    Ttmp = cvA.take([128, 512], F32)
    Sst = cvA.take([128, 512], F32)
    qeT = cvA.take([128, 512], BF16)
    keT = cvA.take([128, 512], BF16)
    ketok2 = [cvA.take([128, 512], BF16) for i in range(2)]
    itok2 = [cvA.take([128, 512], BF16) for i in range(2)]
    dec2 = [cvA.take([128, 8], F32) for i in range(2)]
    AT = cvA.take([128, 512], BF16)
    Sbf = cvA.take([128, 512], BF16)
    kT = [cvA.take([128, 128], BF16) for i in range(2)]
    kTm = cvA.take([128, 16], BF16)
    vz = [[cvA.take([128, 128], BF16) for h in range(2)] for i in range(2)]
    vzm = [cvA.take([128, 128], BF16) for h in range(2)]
    PT = [cvA.take([128, 512], BF16) for i in range(3)]
    cvS = Carver()
    wk = cvS.take([128, 8, 128], BF16)
    wkT = cvS.take([128, 8, 128], BF16)
    wvz = [cvS.take([128, 8, 128], BF16) for h in range(2)]
    mkb = cvS.take([128, 2, 128], BF16)
    mkT = cvS.take([128, 2, 128], BF16)
    mvz = [cvS.take([128, 2, 128], BF16) for h in range(2)]
    pw = cvS.take([128, 64], BF16)
    pm = cvS.take([128, 2, 64], BF16)
    mblk = cvS.take([128, 64], BF16)
    fs = cvS.take([128, 3, 64], F32)
    isf = cvS.take([NS, 512], F32)
    iblk = cvS.take([NS, 512], F32)
    tmpS = [cvS.take([128, 128], F32) for i in range(2)]
    cvB = Carver()
    rT = cvB.take([128, 512], BF16)
    aT = cvB.take([128, 32, 128], BF16)
    gf = cvB.take([128, D], F32)

    T0 = pst("T0", [128, 8, 128], BF16)
    T1 = pst("T1", [128, 8, 128], BF16)
    B = [pst(f"B{i}", [128, 512]) for i in range(6)]

    def mm(out, lhsT, rhs, start, stop, r, w, skip=False):
        if skip:
            S.add('pe', lambda e: e.matmul(out, lhsT, rhs, start=start, stop=stop, skip_group_check=True), r, w)
        else:
            S.add('pe', lambda e: e.matmul(out, lhsT, rhs, start=start, stop=stop), r, w)

    def tr(out, in_, idt, r, w):
        S.add('pe', lambda e: e.transpose(out, in_, idt), r, w)

    def act(out, in_, func, r, w, scale=1.0, bias=0.0):
        S.add('act', lambda e: e.activation(out, in_, func, bias=bias, scale=scale), r, w)

    def ve(eng, name, r, w, *a, **k):
        S.add(eng, lambda e: getattr(e, name)(*a, **k), r, w)

    def dma(q, out, in_, r, w, extra=()):
        return S.add(q, lambda e: e.dma_start(out=out, in_=in_), r, w, dma=True, extra=extra)

    def v4(t):
        return t.rearrange("p (c t) -> p c t", c=4)

    def v8(t):
        return t.rearrange("p (c t) -> p c t", c=8)

    dma('sp', gT[:, :, :], gains[:, :, :], [], ["gT"])
    dma('sp', gbt[:, :], gains_m[:, :], [], ["gbt"])
    dma('sp', lbp_t[:], lbp[:, :], [], ["lbp"])
    dma('sp', prm[:, 12:13], hgn[:, :], [], ["prm12"])
    dma('sp', prm[:, 13:14], flag[:, :], [], ["prm13"])
    dma('sp', snk_t[:], snk[:, :], [], ["snk"])
    if do_sample:
        dma('sp', ksn[:, 0:127, :], ck[:, 1:128, :], [], ["ksn_lo"])
        dma('sp', vsn[:, 0:127, :], cv[:, 1:128, :], [], ["vsn_lo"])
    ve('pool', 'memset', [], ["ones_b"], ones_b[:], 1.0)
    ve('pool', 'memset', [], ["ones_f"], ones_f[:], 1.0)
    ve('pool', 'memset', [], ["zeros_f"], zeros_f[:], 0.0)
    ve('pool', 'memset', [], ["mhalf"], mhalf[:], -0.5)
    for h in range(2):
        ve('pool', 'memset', [], [f"oz{h}"], oz[h][:], 0.0)
        ve('pool', 'memset', [f"oz{h}"], [f"oz{h}"], oz[h][:, h * 64:(h + 1) * 64], 1.0)
    for i in range(2):
        for h in range(2):
            ve('pool', 'memset', [], [f"vz{i}_{h}"], vz[i][h], 0.0)
    for h in range(2):
        ve('pool', 'memset', [], [f"vzm{h}"], vzm[h], 0.0)
    S.add('pool', lambda e: e.affine_select(ident_f[:], ones_f[:], [[-1, 128]], ALU.is_equal, 0.0,
                                            base=0, channel_multiplier=1), ["ones_f"], ["ident_f"])
    ve('pool', 'tensor_copy', ["ident_f"], ["ident_b"], ident_b[:], ident_f[:])
    S.add('pool', lambda e: e.affine_select(m_cur[:], ones_b[:], [[0, 4], [1, 128]], ALU.is_ge, 0.0,
                                            base=0, channel_multiplier=-1), ["ones_b"], ["m_cur"])
    S.add('pool', lambda e: e.affine_select(m_prev[:], ones_b[:], [[0, 4], [-1, 128]], ALU.is_gt, 0.0,
                                            base=0, channel_multiplier=1), ["ones_b"], ["m_prev"])
    ve('pool', 'tensor_scalar', ["m_prev", "prm13"], ["m_prev0"], m_prev0[:], m_prev[:], prm[:, 13:14], 1.0, ALU.mult, ALU.mult)
    S.add('pool', lambda e: e.affine_select(m_hg[:], ones_b[:], [[0, 4], [1, 128]], ALU.is_ge, 0.0,
                                            base=0, channel_multiplier=-1), ["ones_b"], ["m_hg"])
    ve('pool', 'memset', ["m_hg"], ["m_hg"], v4(m_hg[:])[0:64, :, 64:128], 0.0)
    ve('dve', 'tensor_sub', ["lbp"], ["prm0"], prm[:, 0:4], lbp_t[:, 0:4], lbp_t[:, 4:8])
    act(prm[:, 16:20], prm[:, 0:4], AF.Tanh, ["prm0"], ["prm16"], scale=0.5)
    ve('dve', 'tensor_scalar', ["prm16"], ["prm4"], prm[:, 4:8], prm[:, 16:20], -0.25, 0.25, ALU.mult, ALU.add)
    ve('dve', 'tensor_scalar', ["prm16"], ["prm0"], prm[:, 0:4], prm[:, 16:20], 0.25, 0.75, ALU.mult, ALU.add)
    ve('dve', 'tensor_scalar', ["prm16"], ["prm8"], prm[:, 8:12], prm[:, 16:20], 0.25, -0.25, ALU.mult, ALU.add)
    ve('dve', 'tensor_scalar', ["prm12"], ["prm14"], prm[:, 14:15], prm[:, 12:13], 0.5, None, ALU.mult)
    act(snk_t[:], snk_t[:], AF.Exp, ["snk"], ["snk"])
    for g in range(4):
        ve('dve', 'tensor_scalar', ["snk", "ones_f"], ["esink"], v4(esink[:])[:, g, :], ones_f[:], snk_t[:, g:g + 1], None, ALU.mult)
    ve('dve', 'memset', [], ["prm15"], prm[:, 15:16], EPS)
    ve('dve', 'memset', [], ["Sst"], Sst, 0.0)
    ve('dve', 'memset', [], ["Sbf"], Sbf, 0.0)
    ve('dve', 'memset', [], ["cp"], cp, 1.0)
    ve('dve', 'memset', [], ["t1m"], t1m[:], 1.0)

    if stop == 1:
        return finish()
    def wload(dst3, src, c0, c1, key, extra=()):
        return dma('pool', dst3[:, :, c0:c1], src.rearrange("(k p) n -> p k n", p=128)[:, :, c0:c1], [], [key], extra=extra)

    wload(WIN, w_in, OFH, OIH, "w_fh")
    wload(WIN, w_in, OIH, OGH, "w_ih")
    wload(WIN, w_in, OQ, OQH, "w_qkv")
    wload(WIN, w_in, OQH, OFH, "w_qh")
    wload(WIN, w_in, OGH, OGA, "w_gh")
    wload(WIN, w_in, OGA, OGA + 512, "w_ga0")
    wload(WIN, w_in, OGA + 512, OGB, "w_ga1")
    wload(WIN, w_in, OGB, OGB + 512, "w_gb0")
    wload(WIN, w_in, OGB + 512, DIN, "w_gb1")
    wload(WAO, w_ao, 0, D, "w_ao")
    wload(WHO, w_ho, 0, D, "w_ho")
    wload(WO, w_o, 0, D, "w_o")
    WKEYS = ["w_fh", "w_ih", "w_qkv", "w_qh", "w_gh", "w_ga0", "w_ga1", "w_gb0", "w_gb1", "w_ao", "w_ho", "w_o"]

    if stop == 2:
        return finish()
    xcnt = [0]

    class Loader:
        def __init__(self, items):
            self.items = items
            self.n_issued = 0

        def get(self, i, ahead=1):
            while self.n_issued <= min(i + ahead, len(self.items) - 1):
                j = self.n_issued
                src, n, rk = self.items[j]
                dma('sp', xt[j % 2][0:n, :], src, rk, [f"xt{j % 2}"])
                self.n_issued += 1
            return xt[i % 2], f"xt{i % 2}"

    def rstd_rows(x, xk, n, junk=None):
        jt, jk = junk or (xn, "xn")
        S.add('dve', lambda e: e.scalar_tensor_tensor(jt[0:n, :], x[0:n, :], 1.0, x[0:n, :], ALU.mult, ALU.mult,
                                                      accum_out=ss[0:n, 0:1]), [xk], [jk, "ss0"])
        ve('dve', 'tensor_scalar', ["ss0"], ["ss1"], ss[0:n, 1:2], ss[0:n, 0:1], 1.0 / D, EPS, ALU.mult, ALU.add)
        ve('pool', 'tensor_tensor', ["ss1", "mhalf"], ["ss2"], ss[0:n, 2:3], ss[0:n, 1:2], mhalf[0:n, 0:1], ALU.pow)

    def front(x, xk, n, gi, xslot=None):
        if xslot is None:
            slot = xcnt[0] % 2
            xcnt[0] += 1
            xT_t, tk = xnT[slot], f"xnT{slot}"
        else:
            xT_t, tk = xslot
        rstd_rows(x, xk, n)
        S.add('dve', lambda e: e.scalar_tensor_tensor(xn[0:n, :], x[0:n, :], ss[0:n, 2:3], gbt[0:n, :],
                                                      ALU.mult, ALU.mult), [xk, "ss2", "gbt"], ["xn"])
        for k in range(8):
            tr(T0[:, k, 0:n], xn[0:n, k * 128:(k + 1) * 128], ident_b[0:n, 0:n], ["xn", "ident_b"], ["T0"])
        act(xT_t[:, :, 0:n], T0[:, :, 0:n], AF.Copy, ["T0"], [tk])
        return xT_t, tk

    def proj_fm(bank, bkey, xT, tk, n, col0, nch, wkey):
        b3 = v4(bank[:])
        for c in range(nch):
            for k in range(8):
                mm(b3[:, c, 0:n], WIN[:, k, col0 + c * 128:col0 + (c + 1) * 128], xT[:, k, 0:n],
                   k == 0, k == 7, [tk, wkey], [bkey])
        return b3

    def proj_tm(out_ap, bkey, xT, tk, n, col0, ncol, wkey):
        for k in range(8):
            mm(out_ap, xT[:, k, 0:n], WIN[:, k, col0:col0 + ncol], k == 0, k == 7, [tk, wkey], [bkey])

    dec34 = [sb(f"dec{i}x", [128, 8]) for i in (2, 3)]
    th3 = sb("th3", [128, 512])
    ketok2 = ketok2 + [hgT[:], attT[:]]
    itok2 = itok2 + [osq[:], kvo[:].bitcast(BF16)]
    dec2 = dec2 + [d[:] for d in dec34]
    def KK(sl):
        return f"ketok{sl}" if sl < 2 else ("hgT", "attT")[sl - 2]

    def IK(sl):
        return f"itok{sl}" if sl < 2 else ("osq", "kvo_k")[sl - 2]

    def IKW(sl):
        return [IK(sl)] + (["kvo_v"] if sl == 3 else [])
    SLK = ["0", "1", "hgT", "attT"]
    SLI = ["0", "1", "osq", "kvo"]
    BS = [dict(th=(th[:], "th"), f=(fk[:, 0, :], "f"), k=(fk[:, 1, :], "k"), cp=(cp, "cp"), rc=(rc, "rc"),
               keT=(keT, "keT"), tsl=0, ib=(B[2], "B2"), fb=(B[0], "B0")),
          dict(th=(sgh[:], "sgh"), f=(mse[:], "mse"), k=(rden[:], "rden"), cp=(t1m[:], "t1m"), rc=(t2m[:], "t2m"),
               keT=(qT[:], "qT"), tsl=4, ib=(B[3], "B3"), fb=(B[1], "B1"))]

    def hg_gates_a(xT, tk, n, bs=None):
        bs = bs or BS[0]
        b3 = proj_fm(bs['fb'][0], bs['fb'][1], xT, tk, n, OFH, 4, "w_fh")
        act(v4(bs['th'][0])[:, :, 0:n], b3[:, :, 0:n], AF.Tanh, [bs['fb'][1]], [bs['th'][1]], scale=0.5)

    def hg_gates_b(n, bs=None):
        bs = bs or BS[0]
        t3 = v4(bs['th'][0])
        tkey = bs['th'][1]
        f3 = v4(bs['f'][0])
        k3 = v4(bs['k'][0])
        for h in range(4):
            ve('dve', 'tensor_scalar', [tkey, "prm0", "prm4"], [bs['f'][1]], f3[:, h, 0:n], t3[:, h, 0:n],
               prm[:, 4 + h:5 + h], prm[:, h:h + 1], ALU.mult, ALU.add)
            ve('dve', 'tensor_scalar', [tkey, "prm8", "prm4"], [bs['k'][1]], k3[:, h, 0:n], t3[:, h, 0:n],
               prm[:, 8 + h:9 + h], prm[:, 4 + h:5 + h], ALU.mult, ALU.add)
        return f3, k3

    def hg_gates(xT, tk, n):
        hg_gates_a(xT, tk, n)
        return hg_gates_b(n)

    hslot = [0]

    def hg_prep_a(xT, tk, gates_done=False, bs=None, nsl=2):
        bs = bs or BS[0]
        sl = hslot[0] % nsl
        hslot[0] += 1
        dec = dec2[sl]
        if not gates_done:
            hg_gates_a(xT, tk, 128, bs)
        f3, k3 = hg_gates_b(128, bs)
        cpt, cpk = bs['cp']
        rct, rck = bs['rc']
        ket, kek = bs['keT']
        c3 = v4(cpt)
        for h in range(4):
            for c in range(2):
                S.add('dve', lambda e, h=h, c=c, c3=c3, f3=f3: e.tensor_tensor_scan(
                    c3[:, h, c * 64:(c + 1) * 64], f3[:, h, c * 64:(c + 1) * 64], zeros_f[:, 0:64], 1.0,
                    ALU.mult, ALU.add), [bs['f'][1], "zeros_f"], [cpk])
        ve('dve', 'reciprocal', [cpk], [rck], rct, cpt)
        ve('dve', 'tensor_tensor', [bs['k'][1], rck], [kek], ket, bs['k'][0], rct, ALU.mult)
        ve('dve', 'tensor_copy', [cpk], [f"dec{sl}"], dec.rearrange("p (h c) -> p h c", h=4),
           cpt.rearrange("p (h c t) -> p h c t", h=4, c=2)[:, :, :, 63])
        return sl

    def hg_prep_a_rev(xT, tk, bs, nsl=4):
        sl = hslot[0] % nsl
        hslot[0] += 1
        dec = dec2[sl]
        f3, k3 = hg_gates_b(128, bs)
        cpt, cpk = bs['cp']
        ket, kek = bs['keT']
        c3 = v4(cpt)
        for h in range(4):
            for c in range(2):
                S.add('dve', lambda e, h=h, c=c, c3=c3, f3=f3: e.tensor_tensor_scan(
                    c3[:, h, c * 64 + 1:(c + 1) * 64], f3[:, h, c * 64:(c + 1) * 64 - 1], zeros_f[:, 0:63], 1.0,
                    ALU.mult, ALU.add), [bs['f'][1], "zeros_f"], [cpk])
        ve('dve', 'tensor_tensor', [bs['k'][1], cpk], [kek], ket, bs['k'][0], cpt, ALU.mult)
        ve('dve', 'tensor_tensor', [cpk, bs['f'][1]], [f"dec{sl}"], dec.rearrange("p (h c) -> p h c", h=4),
           cpt.rearrange("p (h c t) -> p h c t", h=4, c=2)[:, :, :, 63],
           bs['f'][0].rearrange("p (h c t) -> p h c t", h=4, c=2)[:, :, :, 63], ALU.mult)
        return sl

    def hg_state_update_rev(sl, c, ubank, ukey):
        ketok, itok, dec = ketok2[sl], itok2[sl], dec2[sl]
        u3 = v4(ubank[:])
        for h in range(4):
            mm(u3[:, h, :], v4(ketok)[c * 64:(c + 1) * 64, h, :], v4(itok)[c * 64:(c + 1) * 64, h, :],
               True, True, [KK(sl)] + IKW(sl), [ukey])
        for h in range(4):
            dcol = dec[:, h * 2 + c:h * 2 + c + 1]
            S.add('dve', lambda e, h=h, dcol=dcol, u3=u3: e.scalar_tensor_tensor(
                v4(Sst)[:, h, :], v4(Sst)[:, h, :], dcol, u3[:, h, :], ALU.mult, ALU.add),
                ["Sst", ukey, f"dec{sl}"], ["Sst"])

    def hg_prep_i(sl, xT, tk, bs=None):
        bs = bs or BS[0]
        proj_tm(bs['ib'][0][:, :], bs['ib'][1], xT, tk, 128, OIH, 512, "w_ih")
        act(itok2[sl], bs['ib'][0][:], AF.Copy, [bs['ib'][1]], IKW(sl))

    def hg_prep_t(sl, bs=None):
        bs = bs or BS[0]
        o = bs['tsl']
        for h in range(4):
            tr(T1[:, o + h, :], v4(bs['keT'][0])[:, h, :], ident_b[:], [bs['keT'][1], "ident_b"], ["T1"])
        act(v4(ketok2[sl]), T1[:, o:o + 4, :], AF.Copy, ["T1"], [KK(sl)])

    def hg_chunk_prep(xT, tk):
        sl = hg_prep_a(xT, tk)
        hg_prep_t(sl)
        hg_prep_i(sl, xT, tk)
        return sl

    def hg_state_update(sl, c, ubank, ukey):
        ketok, itok, dec = ketok2[sl], itok2[sl], dec2[sl]
        u3 = v4(ubank[:])
        for h in range(4):
            mm(u3[:, h, :], v4(ketok)[c * 64:(c + 1) * 64, h, :], v4(itok)[c * 64:(c + 1) * 64, h, :],
               True, True, [KK(sl)] + IKW(sl), [ukey])
        ve('dve', 'tensor_tensor', ["Sst", ukey], ["Ttmp"], Ttmp, Sst, ubank[:], ALU.add)
        for h in range(4):
            dcol = dec[:, h * 2 + c:h * 2 + c + 1]
            act(v4(Sst)[:, h, :], v4(Ttmp)[:, h, :], AF.Identity, ["Ttmp", f"dec{sl}"], ["Sst"], scale=dcol)
            ve('pool', 'tensor_scalar', ["Ttmp", f"dec{sl}"], ["Sbf"], v4(Sbf)[:, h, :], v4(Ttmp)[:, h, :], dcol, 1.0, ALU.mult, ALU.mult)

    def kv_tile(xT, tk, n, kT_t, kT_key, vz_t, vz_key, want_k_tok, out_k, out_v):
        for k in range(8):
            mm(B[5][:, 0:n], WIN[:, k, OK_:OK_ + 128], xT[:, k, 0:n], k == 0, k == 7, [tk, "w_qkv"], ["B5"])
        for k in range(8):
            mm(B[5][0:n, 128:256], xT[:, k, 0:n], WIN[:, k, OV:OV + 128], k == 0, k == 7, [tk, "w_qkv"], ["B5"])
        if want_k_tok:
            for k in range(8):
                mm(B[5][0:n, 256:384], xT[:, k, 0:n], WIN[:, k, OK_:OK_ + 128], k == 0, k == 7, [tk, "w_qkv"], ["B5"])
        if kT_t is not None:
            act(kT_t[:, 0:n], B[5][:, 0:n], AF.Copy, ["B5"], [kT_key])
            for h in range(2):
                act(vz_t[h][0:n, h * 64:(h + 1) * 64], B[5][0:n, 128 + h * 64:128 + (h + 1) * 64], AF.Copy,
                    ["B5"], [vz_key + str(h)])
        ids = []
        if out_v is not None:
            ve('dve', 'tensor_copy', ["B5"], ["kvo_v"], kvo[0:n, 128:256], B[5][0:n, 128:256])
            ids.append(dma('sp', out_v, kvo[0:n, 128:256], ["kvo_v"], ["out_v"]))
        if want_k_tok:
            ve('dve', 'tensor_copy', ["B5"], ["kvo_k"], kvo[0:n, 0:128], B[5][0:n, 256:384])
            ids.append(dma('sp', out_k, kvo[0:n, 0:128], ["kvo_k"], ["out_k"]))
        return ids

    def hg_gate_prep(n, xT, tk, gbank=None):
        gb, gk = gbank or (B[3], "B3")
        g3 = proj_fm(gb, gk, xT, tk, n, OGH, 4, "w_gh")
        act(v4(th[:])[:, :, 0:n], g3[:, :, 0:n], AF.Tanh, [gk], ["th"], scale=0.5)
        S.add('dve', lambda e: e.scalar_tensor_tensor(v4(sgh[:])[:, :, 0:n], v4(th[:])[:, :, 0:n], 1.0, g3[:, :, 0:n],
                                                      ALU.add, ALU.mult), ["th", gk], ["sgh"])

    def hg_out(ob, okey, n, xT, tk, gate_done=False):
        o3 = v4(ob[:])
        if not gate_done:
            hg_gate_prep(n, xT, tk)
        act(v4(osq[:])[:, :, 0:n], o3[:, :, 0:n], AF.Square, [okey], ["osq"])
        s3 = v4(B[0][:])
        if n == 128:
            mm(B[0][:, :], ones_b[:, 0:128], osq[:, :], True, True, ["osq", "ones_b"], ["B0"])
        else:
            for hh in range(4):
                mm(s3[:, hh, 0:n], ones_b[:, 0:128], v4(osq[:])[:, hh, 0:n], True, True, ["osq", "ones_b"], ["B0"])
        act(v4(mse[:])[:, :, 0:n], s3[:, :, 0:n], AF.Sqrt, ["B0", "prm15"], ["mse"], scale=1.0 / 128, bias=prm[:, 15:16])
        ve('dve', 'reciprocal', ["mse"], ["mse"], v4(mse[:])[:, :, 0:n], v4(mse[:])[:, :, 0:n])
        ve('dve', 'tensor_tensor', [okey, "mse"], ["mse"], v4(mse[:])[:, :, 0:n], o3[:, :, 0:n], v4(mse[:])[:, :, 0:n], ALU.mult)
        S.add('dve', lambda e: e.scalar_tensor_tensor(v4(hgT[:])[:, :, 0:n], v4(mse[:])[:, :, 0:n], prm[:, 14:15],
                                                      v4(sgh[:])[:, :, 0:n], ALU.mult, ALU.mult),
              ["mse", "sgh", "prm14"], ["hgT"])

    def gates_proj(xT, tk, n):
        for (wk0, wk1, col0, tht, thk, ba, bb, ka, kb) in (
                ("w_ga0", "w_ga1", OGA, tha, "tha", B[2], B[3], "B2", "B3"),
                ("w_gb0", "w_gb1", OGB, thb, "thb", B[4], B[5], "B4", "B5")):
            proj_fm(ba, ka, xT, tk, n, col0, 4, wk0)
            proj_fm(bb, kb, xT, tk, n, col0 + 512, 4, wk1)
            act(v8(tht[:])[:, 0:4, 0:n], v4(ba[:])[:, :, 0:n], AF.Tanh, [ka], [thk], scale=0.5)
            act(v8(tht[:])[:, 4:8, 0:n], v4(bb[:])[:, :, 0:n], AF.Tanh, [kb], [thk], scale=0.5)

    def merge_rest(x, xk, n):
        for half in range(2):
            for (W3, wkey, srcT, skey, tht, thk, tm, tmk, bank, bkey) in (
                    (WAO, "w_ao", attT, "attT", tha, "tha", t1m, "t1m", B[2 + half], f"B{2 + half}"),
                    (WHO, "w_ho", hgT, "hgT", thb, "thb", t2m, "t2m", B[4 + half], f"B{4 + half}")):
                o3 = v4(bank[:])
                for c in range(4):
                    m = half * 4 + c
                    for k in range(4):
                        mm(o3[:, c, 0:n], W3[:, k, m * 128:(m + 1) * 128], v4(srcT[:])[:, k, 0:n], k == 0, k == 3,
                           [skey, wkey], [bkey])
                S.add('dve', lambda e, half=half, o3=o3, tht=tht, tm=tm: e.scalar_tensor_tensor(
                    v4(tm[:])[:, :, 0:n], v8(tht[:])[:, half * 4:half * 4 + 4, 0:n], 1.0,
                    o3[:, :, 0:n], ALU.add, ALU.mult), [thk, bkey], [tmk])
            ve('pool', 'tensor_tensor', ["t1m", "t2m"], ["mixT"], v8(mixT[:])[:, half * 4:half * 4 + 4, 0:n],
               v4(t1m[:])[:, :, 0:n], v4(t2m[:])[:, :, 0:n], ALU.add)
        for nb, (bank, bkey) in enumerate(((B[0], "B0"), (B[1], "B1"))):
            for k in range(8):
                mm(bank[0:n, :], v8(mixT[:])[:, k, 0:n], WO[:, k, nb * 512:(nb + 1) * 512], k == 0, k == 7,
                   ["mixT", "w_o"], [bkey])
            S.add('dve', lambda e, bank=bank, nb=nb: e.scalar_tensor_tensor(
                x[0:n, nb * 512:(nb + 1) * 512], bank[0:n, :], 0.5, x[0:n, nb * 512:(nb + 1) * 512],
                ALU.mult, ALU.add), [bkey, xk], [xk])

    itemsA = [(xp[t * 128:(t + 1) * 128, :], 128, []) for t in range(npre)]
    itemsA += [(xmeta[:, :], 16, []), (xhalo[:, :], 128, [])]
    itemsA += [(xmain[t * 128:(t + 1) * 128, :], 128, []) for t in range(nt)]
    itemsA += [(xs[:, :], NS, [])]
    LA = Loader(itemsA)
    XS = [(xnT[0], "xnT0"), (xnT[1], "xnT1"),
          (tha[:].rearrange("p (k t) -> p k t", k=8), "tha"), (thb[:].rearrange("p (k t) -> p k t", k=8), "thb")]
    THB = [[(th[:], "th"), (sgh[:], "sgh")], [(mixT[:].bitcast(F32), "mixT"), (th3[:], "th3")]]

    def BSG(gidx, i):
        return dict(BS[i], th=THB[gidx % 2][i])

    def pre_a(t, bs):
        x, xk = LA.get(t)
        xT, tk = front(x, xk, 128, 0, xslot=XS[t % 4])
        hg_gates_a(xT, tk, 128, bs)
        return xT, tk

    def pre_b(xT, tk, bs):
        sl = hg_prep_a_rev(xT, tk, bs, nsl=4)
        hg_prep_t(sl, bs)
        hg_prep_i(sl, xT, tk, bs)
        return sl

    def rec_pre_b_group(alist, gidx):
        if len(alist) == 1:
            S.rec = []
            sl0 = pre_b(*alist[0], BSG(gidx, 0))
            l = S.rec
            S.rec = None
            return [sl0], l
        S.rec = []
        sl0 = pre_b(*alist[0], BSG(gidx, 0))
        l0 = S.rec
        S.rec = []
        sl1 = pre_b(*alist[1], BSG(gidx, 1))
        l1 = S.rec
        S.rec = None
        out = []
        i0 = i1 = 0
        while i0 < len(l0) or i1 < len(l1):
            if i0 < len(l0):
                out.append(l0[i0])
                i0 += 1
            if i1 < len(l1):
                out.append(l1[i1])
                i1 += 1
        return [sl0, sl1], out

    def rec_s2(sls):
        S.rec = []
        for sl_cur in sls:
            for c in range(2):
                ub, uk = (B[4], "B4") if c == 0 else (B[5], "B5")
                hg_state_update_rev(sl_cur, c, ub, uk)
        l = S.rec
        S.rec = None
        return l

    def emit_merged(la, lb):
        i1 = i2 = 0
        n1, n2 = len(la), len(lb)
        while i1 < n1 or i2 < n2:
            if i2 >= n2 or (i1 < n1 and i1 * n2 <= i2 * n1):
                S.add(*la[i1])
                i1 += 1
            else:
                S.add(*lb[i2])
                i2 += 1

    pgroups = [list(range(g, min(g + 2, npre))) for g in range(0, npre, 2)]

    def rec_pre_a_group(gidx):
        S.rec = []
        r = [pre_a(tt, BSG(gidx, i)) for i, tt in enumerate(pgroups[gidx])]
        l = S.rec
        S.rec = None
        return r, l

    def emit_merged3(ls):
        ls = [l for l in ls if l]
        idx = [0] * len(ls)
        while True:
            best, bi = None, -1
            for i, l in enumerate(ls):
                if idx[i] < len(l):
                    frac = idx[i] / len(l)
                    if best is None or frac < best:
                        best, bi = frac, i
            if bi < 0:
                break
            S.add(*ls[bi][idx[bi]])
            idx[bi] += 1

    G = len(pgroups)
    a_res = {}
    pend_sl = {}
    if G > 0:
        a_res[0], la0 = rec_pre_a_group(0)
        emit_merged3([la0])
        pend_sl[0], lb0 = rec_pre_b_group(a_res[0], 0)
        emit_merged3([lb0])
    if G > 1:
        a_res[1], la1 = rec_pre_a_group(1)
        emit_merged3([la1])
    for gi in range(G):
        ls2 = rec_s2(pend_sl[gi])
        lb_, la_ = [], []
        if gi + 1 < G:
            pend_sl[gi + 1], lb_ = rec_pre_b_group(a_res[gi + 1], gi + 1)
        if gi + 2 < G:
            a_res[gi + 2], la_ = rec_pre_a_group(gi + 2)
        emit_merged3([ls2, lb_, la_])
        t = gi
        if do_ffn and 2 <= t < 10:
            j = t - 2
            if j < 4:
                dma('pool', wupb[j * 256:(j + 1) * 256, :], w_up[j * 256:(j + 1) * 256, :], [], [f"wupb{j}"])
            else:
                j -= 4
                dma('pool', wdnb[j * 1024:(j + 1) * 1024, :], w_dn[j * 1024:(j + 1) * 1024, :], [], [f"wdnb{j}"])

    ve('pool', 'tensor_copy', ["Sst"], ["Sbf"], Sbf, Sst)
    fenceA_dma = []
    x, xk = LA.get(npre)
    xT, tk = front(x, xk, 16, 0)
    fenceA_dma += kv_tile(xT, tk, 16, kTm, "kTm", vzm, "vzm", True, mk[:, :], mv[:, :])
    if stop == 3:
        return finish()
    x, xk = LA.get(npre + 1)
    xT, tk = front(x, xk, 128, 0)
    kv_tile(xT, tk, 128, kT[1], "kT1", vz[1], "vz1_", False, None, None)
    if stop == 4:
        return finish()

    fS = []
    pre_done = [False]

    def sample_prefetch():
        if pre_done[0] or not do_sample:
            return
        pre_done[0] = True
        fS.append(dma('sp', kst[0:127, :, :], ck[0:8, 1:128, :].rearrange("b r c -> r b c"), [], ["kst"]))
        fS.append(dma('sp', vst[0:127, :, :], cv[0:8, 1:128, :].rearrange("b r c -> r b c"), [], ["vst"]))
        fS.append(dma('sp', mst[:, 0, :], cmk[0:8, :, :].rearrange("b r c -> (b r) c"), [], ["mst"]))
        fS.append(dma('sp', mst[:, 1, :], cmv[0:8, :, :].rearrange("b r c -> (b r) c"), [], ["mst"]))
        for g in range(2):
            fS.append(dma('sp', Ssm2[g][:, :, :], sth[g * 2:(g + 1) * 2].rearrange("b h d v -> d (b h) v"), [], [f"Ssm{g}"]))

    for t in range(nt):
        if t == nt - 1:
            sample_prefetch()
        cur = t % 2
        prv = 1 - cur
        x, xk = LA.get(npre + 2 + t)
        if t == 0:
            xT, tk = front(x, xk, 128, 0)
        else:
            xT, tk = nxt_front
        last = (t == nt - 1)
        sl = hg_prep_a(xT, tk)
        fenceA_dma += kv_tile(xT, tk, 128, kT[cur], f"kT{cur}", vz[cur], f"vz{cur}_", last,
                              kp[:, :] if last else None, vp[:, :] if last else None)
        hg_prep_i(sl, xT, tk)
        itok = itok2[sl]
        gates_proj(xT, tk, 128)
        proj_fm(B[1], "B1", xT, tk, 128, OQ, 4, "w_qkv")
        act(qT[:], B[1][:], AF.Copy, ["B1"], ["qT"])
        q3s = v4(qT[:])
        groups = [(kTm, "kTm", vzm, "vzm", 16, None, None),
                  (kT[prv], f"kT{prv}", vz[prv], f"vz{prv}_", 128, m_prev0 if t == 0 else m_prev,
                   "m_prev0" if t == 0 else "m_prev"),
                  (kT[cur], f"kT{cur}", vz[cur], f"vz{cur}_", 128, m_cur, "m_cur")]

        def score(i):
            h, gi = divmod(i, 3)
            kt, ktk, vzt, vzk, nk, mask, mkey = groups[gi]
            sbk, sbkey = (B[0], "B0") if (i % 2 == 0) else (B[5], "B5")
            mm(sbk[0:nk, :], kt[h * 64:(h + 1) * 64, 0:nk], q3s[h * 64:(h + 1) * 64, :, :], True, True,
               [ktk, "qT"], [sbkey])

        score(0)
        for i in range(6):
            h, gi = divmod(i, 3)
            kt, ktk, vzt, vzk, nk, mask, mkey = groups[gi]
            sbk, sbkey = (B[0], "B0") if (i % 2 == 0) else (B[5], "B5")
            if i + 1 < 6:
                score(i + 1)
            P = PT[i % 3]
            pkey = f"PT{i % 3}"
            act(P[0:nk, :], sbk[0:nk, :], AF.Exp, [sbkey], [pkey], scale=0.125)
            if mask is not None:
                ve('dve', 'tensor_tensor', [pkey, mkey], [pkey], P[0:nk, :], P[0:nk, :], mask[0:nk, :], ALU.mult)
            mm(B[2][:, :], vzt[h][0:nk, :], P[0:nk, :], i == 0, i == 5, [vzk + str(h), pkey], ["B2"])
            mm(B[3][:, :], oz[h][0:nk, :], P[0:nk, :], i == 0, i == 5, [f"oz{h}", pkey], ["B3"])
        ve('dve', 'tensor_tensor', ["B3", "esink"], ["rden"], rden[:], B[3][:], esink[:], ALU.add)
        ve('dve', 'reciprocal', ["rden"], ["rden"], rden[:], rden[:])
        ve('dve', 'tensor_tensor', ["B2", "rden"], ["attT"], attT[:], B[2][:], rden[:], ALU.mult)
        hg_prep_t(sl)
        q3 = proj_fm(B[1], "B1", xT, tk, 128, OQH, 4, "w_qh")
        S.add('dve', lambda e, q3=q3: e.scalar_tensor_tensor(v4(qeT), q3, 128 ** -0.5, v4(cp),
                                                            ALU.mult, ALU.mult), ["B1", "cp"], ["qeT"])
        a3 = v4(B[4][:])
        for h in range(4):
            mm(a3[:, h, :], v4(keT)[:, h, :], v4(qeT)[:, h, :], True, True, ["keT", "qeT"], ["B4"])
        ve('dve', 'tensor_tensor', ["B4", "m_hg"], ["AT"], AT, B[4][:], m_hg[:], ALU.mult)
        o3 = v4(B[1][:])
        for c in range(2):
            for h in range(4):
                osl = o3[:, h, c * 64:(c + 1) * 64]
                mm(osl, v4(Sbf)[:, h, :], v4(qeT)[:, h, c * 64:(c + 1) * 64], True, False, ["Sbf", "qeT"], ["B1"])
                mm(osl, v4(itok)[:, h, :], v4(AT)[:, h, c * 64:(c + 1) * 64], False, True, [IK(sl), "AT"], ["B1"])
            hg_state_update(sl, c, B[5], "B5")
        hg_out(B[1], "B1", 128, xT, tk)
        if t + 1 < nt:
            nx, nxk = LA.get(npre + 2 + t + 1, ahead=0)
            nxt_front = front(nx, nxk, 128, 0)
        merge_rest(x, xk, 128)
        dma('sp', h1s[t * 128:(t + 1) * 128, :], x[:, :], [xk], [f"h1s{t}"])
    fenceA_dma.append(dma('sp', spo.rearrange("h d v -> d h v"), v4(Sst), ["Sst"], ["spo"]))

    x_s = xk_s = None
    if do_sample:
        S.fence(fenceA_dma)
        n = NS
        x_s, xk_s = LA.get(npre + 2 + nt)
        xT, tk = front(x_s, xk_s, n, 0)
        for k in range(8):
            mm(B[5][0:n, 0:128], xT[:, k, 0:n], WIN[:, k, OK_:OK_ + 128], k == 0, k == 7, [tk, "w_qkv"], ["B5"])
        for k in range(8):
            mm(B[5][0:n, 128:256], xT[:, k, 0:n], WIN[:, k, OV:OV + 128], k == 0, k == 7, [tk, "w_qkv"], ["B5"])
        ve('dve', 'tensor_copy', ["B5"], ["kvo_k"], kvo[0:n, 0:128], B[5][0:n, 0:128])
        ve('dve', 'tensor_copy', ["B5"], ["kvo_v"], kvo[0:n, 128:256], B[5][0:n, 128:256])
        dma('sp', ksn[:, 127, :], kvo[0:n, 0:128], ["kvo_k"], ["ksn127"])
        dma('sp', vsn[:, 127, :], kvo[0:n, 128:256], ["kvo_v"], ["vsn127"])
        qb3 = proj_fm(B[4], "B4", xT, tk, n, OQ, 4, "w_qkv")
        act(v4(qT[:])[:, :, 0:n], qb3[:, :, 0:n], AF.Copy, ["B4"], ["qT"])
        f3, k3 = hg_gates(xT, tk, n)
        qh3 = proj_fm(B[1], "B1", xT, tk, n, OQH, 4, "w_qh")
        fs4 = fs.rearrange("p a (c t) -> p a c t", c=4)
        ve('dve', 'tensor_copy', ["f"], ["fs"], fs4[:, 0, :, :], f3[:, :, 0:n])
        ve('dve', 'tensor_copy', ["k"], ["fs"], fs4[:, 1, :, :], k3[:, :, 0:n])
        ve('dve', 'tensor_scalar', ["B1"], ["fs"], fs4[:, 2, :, :], qh3[:, :, 0:n], 128 ** -0.5, None, ALU.mult)
        proj_tm(B[4][0:n, :], "B4", xT, tk, n, OIH, 512, "w_ih")
        ve('dve', 'tensor_copy', ["B4"], ["isf"], isf, B[4][0:n, :])
        hg_gate_prep(n, xT, tk)
        gates_proj(xT, tk, n)
        wup_early = False
        if do_ffn:
            wup_early = True
            ext_in = []
            for kx in WKEYS[:9]:
                ext_in += S.readers(kx)
            for j in range(8):
                dma('sp', WUP[:, :, j * 512:(j + 1) * 512], wupb.rearrange("(k p) n -> p k n", p=128)[:, :, j * 512:(j + 1) * 512],
                    [f"wupb{i}" for i in range(4)], [f"w_up{j}"], extra=ext_in)
        S.add('pool', lambda e: e.affine_select(mblk, ones_b[:, 0:64], [[0, 4], [-16, 16]], ALU.is_ge, 0.0,
                                                base=0, channel_multiplier=1), ["ones_b"], ["mblk"])
        S.add('pool', lambda e: e.affine_select(mblk, mblk, [[0, 4], [16, 16]], ALU.is_ge, 0.0,
                                                base=15, channel_multiplier=-1), ["mblk"], ["mblk"])
        mb3 = mblk.rearrange("p (g b) -> p g b", g=4)[:, :, 0:8]
        pm4 = pm.rearrange("p h (g b) -> p h g b", g=4)
        n3 = v4(B[2][:])
        d3 = v4(B[3][:])
        for h in range(2):
            ve('pool', 'memset', [], [f"wvz{h}"], wvz[h], 0.0)
            ve('pool', 'memset', [], [f"mvz{h}"], mvz[h], 0.0)
        for hb in range(2):
            b0 = hb * 8
            if hb == 0:
                sample_prefetch()
            else:
                fS.append(dma('sp', kst[0:127, :, :], ck[b0:b0 + 8, 1:128, :].rearrange("b r c -> r b c"), [], ["kst"]))
                fS.append(dma('sp', vst[0:127, :, :], cv[b0:b0 + 8, 1:128, :].rearrange("b r c -> r b c"), [], ["vst"]))
                fS.append(dma('sp', mst[:, 0, :], cmk[b0:b0 + 8, :, :].rearrange("b r c -> (b r) c"), [], ["mst"]))
                fS.append(dma('sp', mst[:, 1, :], cmv[b0:b0 + 8, :, :].rearrange("b r c -> (b r) c"), [], ["mst"]))
            fS.append(dma('sp', kst[127:128, :, :], ksn[b0:b0 + 8, 127:128, :].rearrange("b r c -> r b c"), ["ksn127"], ["kst"]))
            fS.append(dma('sp', vst[127:128, :, :], vsn[b0:b0 + 8, 127:128, :].rearrange("b r c -> r b c"), ["vsn127"], ["vst"]))
            ve('dve', 'tensor_copy', ["kst"], ["wk"], wk, kst)
            ve('pool', 'tensor_copy', ["mst"], ["mkb"], mkb[:, hb, :], mst[:, 0, :])
            for h in range(2):
                ve('pool', 'tensor_copy', ["vst", f"wvz{h}"], [f"wvz{h}"], wvz[h][:, :, h * 64:(h + 1) * 64], vst[:, :, h * 64:(h + 1) * 64])
                ve('pool', 'tensor_copy', ["mst", f"mvz{h}"], [f"mvz{h}"], mvz[h][:, hb, h * 64:(h + 1) * 64], mst[:, 1, h * 64:(h + 1) * 64])
            for j in range(8):
                tr(T0[:, j, :], wk[:, j, :], ident_b[:], ["wk", "ident_b"], ["T0"])
            act(wkT[:, :, :], T0[:, :, :], AF.Copy, ["T0"], ["wkT"])
            tr(T1[:, 0, :], mkb[:, hb, :], ident_b[:], ["mkb", "ident_b"], ["T1"])
            act(mkT[:, hb, :], T1[:, 0, :], AF.Copy, ["T1"], ["mkT"])
            sbank = ((B[0], "B0"), (B[4], "B4"))
            mbank = ((B[1], "B1"), (B[5], "B5"))
            for j in range(8):
                b = b0 + j
                for h in range(2):
                    mm(sbank[h][0][:, j * 4:j * 4 + 4], wkT[h * 64:(h + 1) * 64, j, :], v4(qT[:])[h * 64:(h + 1) * 64, :, b], True, True,
                       ["wkT", "qT"], [sbank[h][1]])
            for h in range(2):
                m3 = mbank[h][0][:, 0:64].rearrange("p (g b) -> p g b", g=4)
                for g in range(4):
                    mm(m3[:, g, 0:8], mkT[h * 64:(h + 1) * 64, hb, :], v4(qT[:])[h * 64:(h + 1) * 64, g, b0:b0 + 8],
                       True, True, ["mkT", "qT"], [mbank[h][1]])
            for h in range(2):
                act(pw[:, h * 32:(h + 1) * 32], sbank[h][0][:, 0:32], AF.Exp, [sbank[h][1]], ["pw"], scale=0.125)
                m3 = mbank[h][0][:, 0:64].rearrange("p (g b) -> p g b", g=4)
                act(pm4[:, h, :, 0:8], m3[:, :, 0:8], AF.Exp, [mbank[h][1]], ["pm"], scale=0.125)
                ve('pool', 'tensor_tensor', ["pm", "mblk"], ["pm"], pm4[:, h, :, 0:8], pm4[:, h, :, 0:8], mb3, ALU.mult)
            for h in range(2):
                for g in range(4):
                    st0 = (hb == 0 and h == 0 and g == 0)
                    mm(n3[:, g, b0:b0 + 8], mvz[h][:, hb, :], pm4[:, h, g, 0:8], st0, False, [f"mvz{h}", "pm"], ["B2"], skip=True)
                    mm(d3[:, g, b0:b0 + 8], oz[h][:, :], pm4[:, h, g, 0:8], st0, False, [f"oz{h}", "pm"], ["B3"], skip=True)
            for j in range(8):
                b = b0 + j
                for h in range(2):
                    col = h * 32 + j * 4
                    lastw = (hb == 1 and j == 7 and h == 1)
                    mm(n3[:, :, b], wvz[h][:, j, :], pw[:, col:col + 4], False, lastw, [f"wvz{h}", "pw"], ["B2"], skip=True)
                    mm(d3[:, :, b], oz[h][:, :], pw[:, col:col + 4], False, lastw, [f"oz{h}", "pw"], ["B3"], skip=True)
        ve('dve', 'tensor_tensor', ["B3", "esink"], ["rden"], v4(rden[:])[:, :, 0:n], d3[:, :, 0:n], v4(esink[:])[:, :, 0:n], ALU.add)
        ve('dve', 'reciprocal', ["rden"], ["rden"], v4(rden[:])[:, :, 0:n], v4(rden[:])[:, :, 0:n])
        ve('dve', 'tensor_tensor', ["B2", "rden"], ["attT"], v4(attT[:])[:, :, 0:n], n3[:, :, 0:n], v4(rden[:])[:, :, 0:n], ALU.mult)
        os3 = v4(B[5][:])
        iblk2 = [iblk, iblk_b]

        def bcast(bb):
            ibl, iblk_k = iblk2[bb % 2], f"iblk{bb % 2}"
            ve('dve', 'tensor_scalar', ["isf", "ident_f"], [iblk_k], ibl, isf, ident_f[0:n, bb:bb + 1], None, ALU.mult)
            ib_, ibk_ = (B[0], "B0") if (bb % 2 == 0) else (B[2], "B2")
            mm(ib_[:, :], ones_f[0:n, 0:128], ibl, True, True, [iblk_k, "ones_f"], [ibk_])

        bcast(0)
        for g in range(8):
            Ssm = Ssm2[g % 2]
            sk = f"Ssm{g % 2}"
            if g >= 2:
                fS.append(dma('sp', Ssm[:, :, :], sth[g * 2:(g + 1) * 2].rearrange("b h d v -> d (b h) v"), [], [sk]))
            for j in range(2):
                b = g * 2 + j
                for h in range(4):
                    act(Ssm[:, j * 4 + h, :], Ssm[:, j * 4 + h, :], AF.Identity, [sk, "fs"], [sk], scale=fs4[:, 0, h, b:b + 1])
            for j in range(2):
                b = g * 2 + j
                if b + 1 < n:
                    bcast(b + 1)
                ib, ibk = (B[0], "B0") if (b % 2 == 0) else (B[2], "B2")
                for h in range(4):
                    S.add('dve', lambda e, Ssm=Ssm, ib=ib, j=j, h=h, b=b: e.scalar_tensor_tensor(
                        Ssm[:, j * 4 + h, :], ib[:, h * 128:(h + 1) * 128], fs4[:, 1, h, b:b + 1], Ssm[:, j * 4 + h, :],
                        ALU.mult, ALU.add), [sk, "fs", ibk], [sk])
                    mm(os3[:, h, b:b + 1], Ssm[:, j * 4 + h, :], fs4[:, 2, h, b:b + 1], True, True, [sk, "fs"], ["B5"])
            fS.append(dma('sp', ssn[g * 2:(g + 1) * 2].rearrange("b h d v -> d (b h) v"), Ssm[:, :, :], [sk], [f"ssn{g}"]))
        hg_out(B[5], "B5", n, xT, tk, gate_done=True)
        merge_rest(x_s, xk_s, n)

    if do_ffn:
        S.fence(fS)
        ext = []
        for kx in WKEYS:
            ext += S.readers(kx)
        dma('sp', gf, gains_f[:, :], [], ["gf"])
        dma('sp', gbt[:, :], gains_b[:, :], [], ["gbt"])
        for j in range(8):
            if do_sample and wup_early:
                break
            dma('sp', WUP[:, :, j * 512:(j + 1) * 512], wupb.rearrange("(k p) n -> p k n", p=128)[:, :, j * 512:(j + 1) * 512],
                [f"wupb{i}" for i in range(4)], [f"w_up{j}"], extra=ext)
        for j in range(8):
            dma('sp', WDN[:, j * 4:(j + 1) * 4, :], wdnb.rearrange("(k p) n -> p k n", p=128)[:, j * 4:(j + 1) * 4, :],
                [f"wdnb{j // 2}"], [f"w_dn{j}"], extra=ext)

        def ffn_up(xT, tk, n):
            for m4 in range(8):
                bank, bkey = (B[2 + m4 % 4], f"B{2 + m4 % 4}")
                b3 = v4(bank[:])
                for c in range(4):
                    m = m4 * 4 + c
                    for k in range(8):
                        mm(b3[:, c, 0:n], WUP[:, k, m * 128:(m + 1) * 128], xT[:, k, 0:n], k == 0, k == 7,
                           [tk, f"w_up{m // 4}"], [bkey])
                act(v4(rT)[:, :, 0:n], b3[:, :, 0:n], AF.Relu, [bkey], ["rT"])
                ve('pool', 'tensor_tensor', ["rT"], ["aT"], aT[:, m4 * 4:(m4 + 1) * 4, 0:n], v4(rT)[:, :, 0:n], v4(rT)[:, :, 0:n], ALU.mult)

        def ffn_down(x, xk, n, out_rows):
            for nb, (bank, bkey) in enumerate(((B[0], "B0"), (B[1], "B1"))):
                for k in range(32):
                    mm(bank[0:n, :], aT[:, k, 0:n], WDN[:, k, nb * 512:(nb + 1) * 512], k == 0, k == 31,
                       ["aT", f"w_dn{k // 4}"], [bkey])
                ve('dve', 'tensor_tensor', [bkey, xk], [xk], x[0:n, nb * 512:(nb + 1) * 512], bank[0:n, :],
                   x[0:n, nb * 512:(nb + 1) * 512], ALU.add)
            rstd_rows(x, xk, n, junk=(mixT, "mixT"))
            S.add('dve', lambda e: e.scalar_tensor_tensor(x[0:n, :], x[0:n, :], ss[0:n, 2:3], gf[0:n, :],
                                                          ALU.mult, ALU.mult), [xk, "ss2", "gf"], [xk])
            dma('sp', out_rows, x[0:n, :], [xk], [])

        LB = Loader([(h1s[t * 128:(t + 1) * 128, :], 128, [f"h1s{t}"]) for t in range(nt)])
        tiles = []
        if do_sample:
            tiles.append(("s", None))
        tiles += [("m", t) for t in range(nt)]

        def tile_x(tl):
            if tl[0] == "s":
                return x_s, xk_s, NS, ys[:, :]
            xx, xxk = LB.get(tl[1], ahead=0)
            return xx, xxk, 128, y[tl[1] * 128:(tl[1] + 1) * 128, :]

        cur_t = tile_x(tiles[0]) if tiles else None
        cur_f = front(cur_t[0], cur_t[1], cur_t[2], 1) if tiles else None
        for i, tl in enumerate(tiles):
            x, xk, n, orow = cur_t
            xT, tk = cur_f
            ffn_up(xT, tk, n)
            if i + 1 < len(tiles):
                nxt_t = tile_x(tiles[i + 1])
                nxt_f = front(nxt_t[0], nxt_t[1], nxt_t[2], 1)
            ffn_down(x, xk, n, orow)
            if i + 1 < len(tiles):
                cur_t, cur_f = nxt_t, nxt_f

    return finish()


_CACHE = {}


def _prep_inputs(inp):
    f32 = np.float32
    xpmt = np.asarray(inp['x_prompt'], f32)
    xsm = np.asarray(inp['x_sample'], f32)
    meta = np.asarray(inp['meta'], f32)
    w_in = np.asarray(inp['w_in'], f32)[0]
    perm = np.array([(4 * h + g) * 64 + d for g in range(4) for h in range(2) for d in range(64)])
    w_in_p = np.ascontiguousarray(np.concatenate([w_in[:, perm], w_in[:, 512:]], axis=1))
    w_ao_p = np.ascontiguousarray(np.asarray(inp['w_att_out'], f32)[0][perm, :])
    w_ho = np.ascontiguousarray(np.asarray(inp['w_hg_out'], f32)[0])
    w_o = np.ascontiguousarray(np.asarray(inp['w_o'], f32)[0])
    w_up = np.ascontiguousarray(np.asarray(inp['w_up'], f32)[0])
    w_dn = np.ascontiguousarray(np.asarray(inp['w_down'], f32)[0])
    g3 = np.stack([np.asarray(inp['ln_mix'], f32)[0], np.asarray(inp['ln_ffn'], f32)[0],
                   np.asarray(inp['ln_f'], f32)])
    gains = np.ascontiguousarray(g3.reshape(3, 8, 128).transpose(2, 0, 1))
    gains_f = np.ascontiguousarray(np.broadcast_to(g3[2][None, :], (128, D)))
    gains_m = np.ascontiguousarray(np.broadcast_to(g3[0][None, :], (128, D)))
    gains_b = np.ascontiguousarray(np.broadcast_to(g3[1][None, :], (128, D)))
    lb = np.asarray(inp['lb_param'], f32)
    lbp = np.ascontiguousarray(lb.reshape(2, 4, 128).transpose(2, 0, 1).reshape(128, 8))
    hgn = np.ascontiguousarray(np.asarray(inp['hg_norm'], f32)[0].reshape(128, 1))
    sk = np.asarray(inp['sinks'], f32)[0]
    snk = np.ascontiguousarray(np.repeat(sk.reshape(2, 1, 4), 64, axis=1).reshape(128, 4))
    ck = np.asarray(inp['cache_k'], f32)[0].reshape(128, 128, 128)
    cv = np.asarray(inp['cache_v'], f32)[0].reshape(128, 128, 128)
    cmk = np.asarray(inp['cache_meta_k'], f32)[0].reshape(128, 16, 128)
    cmv = np.asarray(inp['cache_meta_v'], f32)[0].reshape(128, 16, 128)
    sth = np.asarray(inp['state_hgrn'], f32)[0]
    maps = []
    for c in range(NCORES):
        b, j = c // NSEG, c % NSEG
        npad = NPRE * 128 - 16 - TOK * j
        xp = np.concatenate([np.zeros((npad, D), f32), meta, xpmt[b, 0:TOK * j]], axis=0)
        xp = xp.reshape(-1, 64, D)[:, ::-1, :].reshape(-1, D)
        xhalo = xpmt[b, TOK * j - 128:TOK * j] if j > 0 else np.zeros((128, D), f32)
        maps.append(dict(
            xp=np.ascontiguousarray(xp), xmeta=meta, xhalo=np.ascontiguousarray(xhalo),
            xmain=np.ascontiguousarray(xpmt[b, TOK * j:TOK * (j + 1)]),
            xs=np.ascontiguousarray(xsm[NS * c:NS * (c + 1), 0, :]),
            w_in=w_in_p, w_ao=w_ao_p, w_ho=w_ho, w_o=w_o, w_up=w_up, w_dn=w_dn,
            gains=gains, gains_f=gains_f, gains_m=gains_m, gains_b=gains_b, lbp=lbp, hgn=hgn, snk=snk,
            flag=np.full((128, 1), 1.0 if j > 0 else 0.0, f32),
            ck=np.ascontiguousarray(ck[NS * c:NS * (c + 1)]), cv=np.ascontiguousarray(cv[NS * c:NS * (c + 1)]),
            cmk=np.ascontiguousarray(cmk[NS * c:NS * (c + 1)]), cmv=np.ascontiguousarray(cmv[NS * c:NS * (c + 1)]),
            sth=np.ascontiguousarray(sth[NS * c:NS * (c + 1)]),
        ))
    return maps


def kernel(**inp):
    if 'nc' not in _CACHE:
        _CACHE['nc'] = build_program()[0]
    nc = _CACHE['nc']
    maps = _prep_inputs(inp)
    res = run_bass_kernel_spmd(nc, maps, core_ids=list(range(NCORES)))
    R = res.results
    f32 = np.float32
    y_prompt = np.zeros((2, SEQ, D), f32)
    y_sample = np.zeros((128, 1, D), f32)
    nk = np.zeros((1, 2, 128, 2, 64), f32)
    nv = np.zeros((1, 2, 128, 2, 64), f32)
    nmk = np.zeros((1, 2, 16, 2, 64), f32)
    nmv = np.zeros((1, 2, 16, 2, 64), f32)
    nsp = np.zeros((1, 2, 4, 128, 128), f32)
    nks = np.zeros((1, 128, 128, 2, 64), f32)
    nvs = np.zeros((1, 128, 128, 2, 64), f32)
    nss = np.zeros((1, 128, 4, 128, 128), f32)
    for c in range(NCORES):
        b, j = c // NSEG, c % NSEG
        r = R[c]
        y_prompt[b, TOK * j:TOK * (j + 1)] = r['y']
        y_sample[NS * c:NS * (c + 1), 0] = r['ys']
        if j == NSEG - 1:
            nk[0, b] = r['kp'].reshape(128, 2, 64)
            nv[0, b] = r['vp'].reshape(128, 2, 64)
            nsp[0, b] = r['spo']
        if j == 0:
            nmk[0, b] = r['mk'].reshape(16, 2, 64)
            nmv[0, b] = r['mv'].reshape(16, 2, 64)
        nks[0, NS * c:NS * (c + 1)] = r['ksn'].reshape(NS, 128, 2, 64)
        nvs[0, NS * c:NS * (c + 1)] = r['vsn'].reshape(NS, 128, 2, 64)
        nss[0, NS * c:NS * (c + 1)] = r['ssn']
    return (y_prompt, y_sample, nk, nv, nmk, nmv, nsp, nks, nvs, nss)
```

```python
import numpy as np
from contextlib import ExitStack
import concourse.bass as bass
import concourse.mybir as mybir
from concourse.bass_utils import run_bass_kernel_spmd

F32 = mybir.dt.float32
BF16 = mybir.dt.bfloat16
AF = mybir.ActivationFunctionType
ALU = mybir.AluOpType

NCORES = 8
D = 1024
SEQ = 8192
NSEG = 4
TOK = SEQ // NSEG
NT = TOK // 128
NPRE = 49
NS = 16
DIN = 4864
DFF = 4096
EPS = 1e-6
OQ, OK_, OV, OQH, OFH, OIH, OGH, OGA, OGB = 0, 512, 640, 768, 1280, 1792, 2304, 2816, 3840


class Sched:
    EPOCH = 4000
    ROT = 16

    def __init__(self):
        self.ops = []
        self.lw = {}
        self.rd = {}
        self.fdeps = []
        self.limit = None
        self.rec = None

    def fence(self, dma_ops=()):
        last = {}
        for i, o in enumerate(self.ops):
            if not o['dma']:
                last[o['eng']] = i
        self.fdeps = list(last.values()) + list(dma_ops)

    def add(self, eng, fn, reads=(), writes=(), dma=False, extra=()):
        if self.limit is not None and len(self.ops) >= self.limit:
            return 0
        if self.rec is not None:
            self.rec.append((eng, fn, tuple(reads), tuple(writes), dma, tuple(extra)))
            return -1
        deps = {}

        def dep(i):
            o = self.ops[i]
            if o['dma']:
                deps[('d', i)] = i
            else:
                k = ('e', o['eng'])
                if k not in deps or deps[k] < i:
                    deps[k] = i
        for k in reads:
            if k in self.lw:
                dep(self.lw[k])
            if len(k) == 2 and k[0] in 'BT' and k[1].isdigit():
                for rk, i in self.rd.get(k, {}).items():
                    if rk != eng:
                        dep(i)
        for k in writes:
            if k in self.lw:
                dep(self.lw[k])
            for i in self.rd.get(k, {}).values():
                dep(i)
        for i in extra:
            dep(i)
        for i in self.fdeps:
            dep(i)
        idx = len(self.ops)
        self.ops.append(dict(eng=eng, fn=fn, deps=sorted(deps.values()), dma=dma, ms=False))
        rk = ('d', idx) if dma else eng
        for k in reads:
            self.rd.setdefault(k, {})[rk] = idx
        for k in writes:
            self.lw[k] = idx
            self.rd[k] = {}
        return idx

    def readers(self, key):
        r = list(self.rd.get(key, {}).values())
        if key in self.lw:
            r.append(self.lw[key])
        return r

    def prepare(self, nc, stack):
        ops = self.ops
        engs = ['pe', 'act', 'dve', 'pool', 'sp']
        for o in ops:
            for d in o['deps']:
                od = ops[d]
                if od['dma']:
                    continue
                if od['eng'] == 'pe' and o['eng'] == 'pe' and not o['dma']:
                    continue
                od['ms'] = True
        cnt = {e: 0 for e in engs}
        dcnt = {e: 0 for e in engs}
        for o in ops:
            if o['dma']:
                o['dn'] = dcnt[o['eng']]
                dcnt[o['eng']] += 1
            elif o['ms']:
                cnt[o['eng']] += 1
                o['msn'] = cnt[o['eng']]
        sems = {}
        for e in engs:
            n = (cnt[e] + self.EPOCH - 1) // self.EPOCH
            sems[e] = [stack.enter_context(nc.semaphore(f"s_{e}_{i}")) for i in range(max(n, 1))]
        dsems = {}
        for e in engs:
            if dcnt[e]:
                dsems[e] = [stack.enter_context(nc.semaphore(f"d_{e}_{i}")) for i in range(self.ROT)]
        known = {e: {f: 0 for f in engs} for e in engs}
        knownd = {e: {} for e in engs}
        streams = {e: [] for e in engs}

        def ms_wait(o_eng, waits, od):
            m = od['msn']
            if known[o_eng][od['eng']] >= m:
                return
            known[o_eng][od['eng']] = m
            waits.append((sems[od['eng']][(m - 1) // self.EPOCH], (m - 1) % self.EPOCH + 1))

        def d_wait(o_eng, waits, q, n):
            s = dsems[q][n % self.ROT]
            v = 16 * (n // self.ROT + 1)
            if knownd[o_eng].get((q, n % self.ROT), 0) >= v:
                return
            knownd[o_eng][(q, n % self.ROT)] = v
            waits.append((s, v))

        for o in ops:
            e = o['eng']
            waits = []
            for d in o['deps']:
                od = ops[d]
                if od['dma']:
                    d_wait(e, waits, od['eng'], od['dn'])
                else:
                    if od['eng'] == 'pe' and e == 'pe' and not o['dma']:
                        continue
                    ms_wait(e, waits, od)
            inc = None
            if o['dma']:
                n = o['dn']
                if n >= self.ROT:
                    d_wait(e, waits, e, n - self.ROT)
                inc = (dsems[e][n % self.ROT], 16)
            elif o['ms']:
                m = o['msn']
                inc = (sems[e][(m - 1) // self.EPOCH], 1)
            streams[e].append((waits, o['fn'], inc))
        fin = []
        for q in engs:
            for n in range(max(0, dcnt[q] - self.ROT), dcnt[q]):
                fin.append((dsems[q][n % self.ROT], 16 * (n // self.ROT + 1)))

        self.streams, self.fin = streams, fin
        return {e: len(streams[e]) for e in engs}

    def emit(self, block):
        streams, fin = self.streams, self.fin

        def run(eng_obj, name):
            for waits, fn, inc in streams[name]:
                for s, v in waits:
                    eng_obj.wait_ge(s, v)
                ins = fn(eng_obj)
                if inc is not None:
                    ins.then_inc(inc[0], inc[1])
            if name == 'sp':
                for s, v in fin:
                    eng_obj.wait_ge(s, v)

        @block.tensor
        def _(e):
            run(e, 'pe')

        @block.scalar
        def _(e):
            run(e, 'act')

        @block.vector
        def _(e):
            run(e, 'dve')

        @block.gpsimd
        def _(e):
            run(e, 'pool')

        @block.sync
        def _(e):
            run(e, 'sp')


def build_program(npre=NPRE, nt=NT, do_sample=True, do_ffn=True, stop=0, limit=None):
    nc = bass.Bass("TRN2", target_bir_lowering=False)
    S = Sched()
    S.limit = limit
    st = ExitStack()

    def finish():
        counts = S.prepare(nc, st)
        blk = st.enter_context(nc.Block())
        S.emit(blk)
        st.close()
        return nc, counts

    def din(name, shape):
        return nc.dram_tensor(name, list(shape), F32, kind="ExternalInput").ap()

    def dout(name, shape):
        return nc.dram_tensor(name, list(shape), F32, kind="ExternalOutput").ap()

    xp = din("xp", [NPRE * 128, D])
    xmeta = din("xmeta", [16, D])
    xhalo = din("xhalo", [128, D])
    xmain = din("xmain", [TOK, D])
    xs = din("xs", [NS, D])
    w_in = din("w_in", [D, DIN])
    w_ao = din("w_ao", [512, D])
    w_ho = din("w_ho", [512, D])
    w_o = din("w_o", [D, D])
    w_up = din("w_up", [D, DFF])
    w_dn = din("w_dn", [DFF, D])
    gains = din("gains", [128, 3, 8])
    gains_f = din("gains_f", [128, D])
    gains_m = din("gains_m", [128, D])
    gains_b = din("gains_b", [128, D])
    lbp = din("lbp", [128, 8])
    hgn = din("hgn", [128, 1])
    snk = din("snk", [128, 4])
    flag = din("flag", [128, 1])
    ck = din("ck", [NS, 128, 128])
    cv = din("cv", [NS, 128, 128])
    cmk = din("cmk", [NS, 16, 128])
    cmv = din("cmv", [NS, 16, 128])
    sth = din("sth", [NS, 4, 128, 128])

    y = dout("y", [TOK, D])
    ys = dout("ys", [NS, D])
    kp = dout("kp", [128, 128])
    vp = dout("vp", [128, 128])
    mk = dout("mk", [16, 128])
    mv = dout("mv", [16, 128])
    spo = dout("spo", [4, 128, 128])
    ksn = dout("ksn", [NS, 128, 128])
    vsn = dout("vsn", [NS, 128, 128])
    ssn = dout("ssn", [NS, 4, 128, 128])
    h1s = nc.dram_tensor("h1s", [TOK, D], F32, kind="Internal").ap()
    wupb = nc.dram_tensor("wupb", [D, DFF], BF16, kind="Internal").ap()
    wdnb = nc.dram_tensor("wdnb", [DFF, D], BF16, kind="Internal").ap()

    def sb(name, shape, dt=F32):
        return st.enter_context(nc.sbuf_tensor(name, list(shape), dt))

    def pst(name, shape, dt=F32):
        return st.enter_context(nc.psum_tensor(name, list(shape), dt))

    arena = sb("arena", [128, 65536], BF16)
    WIN = arena[:, 0:8 * DIN].rearrange("p (k n) -> p k n", k=8)
    o1 = 8 * DIN
    WAO = arena[:, o1:o1 + 4096].rearrange("p (k n) -> p k n", k=4)
    WHO = arena[:, o1 + 4096:o1 + 8192].rearrange("p (k n) -> p k n", k=4)
    WO = arena[:, o1 + 8192:o1 + 16384].rearrange("p (k n) -> p k n", k=8)
    WUP = arena[:, 0:32768].rearrange("p (k n) -> p k n", k=8)
    t0_ = o1 + 16384

    def tailf(off_b, shape):
        nel = int(np.prod(shape[1:]))
        v = arena[:, t0_ + off_b // 2:t0_ + off_b // 2 + nel * 2].bitcast(F32)
        if len(shape) == 3:
            v = v.rearrange("p (a b) -> p a b", a=shape[1])
        return v
    Ssm2 = [tailf(0, [128, 8, 128]), tailf(4096, [128, 8, 128])]
    kst = tailf(8192, [128, 8, 128])
    vst = tailf(12288, [128, 8, 128])
    mst = tailf(16384, [128, 2, 128])
    iblk_b = tailf(17408, [128, 512])[0:NS]
    WDN = arena[:, 32768:65536].rearrange("p (k n) -> p k n", k=32)

    gT = sb("gT", [128, 3, 8])
    gbt = sb("gbt", [128, D])
    ident_b = sb("ident_b", [128, 128], BF16)
    ident_f = sb("ident_f", [128, 128])
    ones_b = sb("ones_b", [128, 512], BF16)
    ones_f = sb("ones_f", [128, 128])
    zeros_f = sb("zeros_f", [128, 64])
    mhalf = sb("mhalf", [128, 512], BF16)
    oz = [sb(f"oz{h}", [128, 128], BF16) for h in range(2)]
    m_cur = sb("m_cur", [128, 512], BF16)
    m_prev = sb("m_prev", [128, 512], BF16)
    m_prev0 = sb("m_prev0", [128, 512], BF16)
    m_hg = sb("m_hg", [128, 512], BF16)
    prm = sb("prm", [128, 32])
    lbp_t = sb("lbp_t", [128, 8])
    snk_t = sb("snk_t", [128, 4])
    esink = sb("esink", [128, 512])
    xt = [sb(f"xt{i}", [128, D]) for i in range(2)]
    xn = sb("xn", [128, D], BF16)
    ss = sb("ss", [128, 4])
    xnT = [sb(f"xnT{i}", [128, 8, 128], BF16) for i in range(2)]
    th = sb("th", [128, 512])
    fk = sb("fk", [128, 2, 512])
    sgh = sb("sgh", [128, 512])
    osq = sb("osq", [128, 512], BF16)
    mse = sb("mse", [128, 512])
    hgT = sb("hgT", [128, 512], BF16)
    qT = sb("qT", [128, 512], BF16)
    attT = sb("attT", [128, 512], BF16)
    rden = sb("rden", [128, 512])
    kvo = sb("kvo", [128, 256])
    tha = sb("tha", [128, 1024], BF16)
    thb = sb("thb", [128, 1024], BF16)
    t1m = sb("t1m", [128, 512])
    t2m = sb("t2m", [128, 512])
    mixT = sb("mixT", [128, 1024], BF16)
    OVL_B = 21760
    ovl = sb("ovl", [128, OVL_B // 2], BF16)

    class Carver:
        def __init__(self):
            self.off = 0

        def take(self, shape, dt):
            esz = 4 if dt == F32 else 2
            nel = int(np.prod(shape[1:]))
            nb = nel * esz
            assert self.off % 4 == 0
            assert self.off + nb <= OVL_B, (self.off, nb)
            if dt == F32:
                v = ovl[:, self.off // 2:(self.off + nb) // 2].bitcast(F32)
            else:
                v = ovl[:, self.off // 2:(self.off + nb) // 2]
            self.off += nb
            if len(shape) == 3:
                v = v.rearrange("p (a b) -> p a b", a=shape[1])
            return v[0:shape[0]]

    cvA = Carver()
    cp = cvA.take([128, 512], F32)
    rc = cvA.take([128, 512], F32)
    Ttmp = cvA.take([128, 512], F32)
    Sst = cvA.take([128, 512], F32)
    qeT = cvA.take([128, 512], BF16)
    keT = cvA.take([128, 512], BF16)
    ketok2 = [cvA.take([128, 512], BF16) for i in range(2)]
    itok2 = [cvA.take([128, 512], BF16) for i in range(2)]
    dec2 = [cvA.take([128, 8], F32) for i in range(2)]
    AT = cvA.take([128, 512], BF16)
    Sbf = cvA.take([128, 512], BF16)
    kT = [cvA.take([128, 128], BF16) for i in range(2)]
    kTm = cvA.take([128, 16], BF16)
    vz = [[cvA.take([128, 128], BF16) for h in range(2)] for i in range(2)]
    vzm = [cvA.take([128, 128], BF16) for h in range(2)]
    PT = [cvA.take([128, 512], BF16) for i in range(3)]
    cvS = Carver()
    wk = cvS.take([128, 8, 128], BF16)
    wkT = cvS.take([128, 8, 128], BF16)
    wvz = [cvS.take([128, 8, 128], BF16) for h in range(2)]
    mkb = cvS.take([128, 2, 128], BF16)
    mkT = cvS.take([128, 2, 128], BF16)
    mvz = [cvS.take([128, 2, 128], BF16) for h in range(2)]
    pw = cvS.take([128, 64], BF16)
    pm = cvS.take([128, 2, 64], BF16)
    mblk = cvS.take([128, 64], BF16)
    fs = cvS.take([128, 3, 64], F32)
    isf = cvS.take([NS, 512], F32)
    iblk = cvS.take([NS, 512], F32)
    tmpS = [cvS.take([128, 128], F32) for i in range(2)]
    cvB = Carver()
    rT = cvB.take([128, 512], BF16)
    aT = cvB.take([128, 32, 128], BF16)
    gf = cvB.take([128, D], F32)

    T0 = pst("T0", [128, 8, 128], BF16)
    T1 = pst("T1", [128, 8, 128], BF16)
    B = [pst(f"B{i}", [128, 512]) for i in range(6)]

    def mm(out, lhsT, rhs, start, stop, r, w, skip=False):
        if skip:
            S.add('pe', lambda e: e.matmul(out, lhsT, rhs, start=start, stop=stop, skip_group_check=True), r, w)
        else:
            S.add('pe', lambda e: e.matmul(out, lhsT, rhs, start=start, stop=stop), r, w)

    def tr(out, in_, idt, r, w):
        S.add('pe', lambda e: e.transpose(out, in_, idt), r, w)

    def act(out, in_, func, r, w, scale=1.0, bias=0.0):
        S.add('act', lambda e: e.activation(out, in_, func, bias=bias, scale=scale), r, w)

    def ve(eng, name, r, w, *a, **k):
        S.add(eng, lambda e: getattr(e, name)(*a, **k), r, w)

    def dma(q, out, in_, r, w, extra=()):
        return S.add(q, lambda e: e.dma_start(out=out, in_=in_), r, w, dma=True, extra=extra)

    def v4(t):
        return t.rearrange("p (c t) -> p c t", c=4)

    def v8(t):
        return t.rearrange("p (c t) -> p c t", c=8)

    dma('sp', gT[:, :, :], gains[:, :, :], [], ["gT"])
    dma('sp', gbt[:, :], gains_m[:, :], [], ["gbt"])
    dma('sp', lbp_t[:], lbp[:, :], [], ["lbp"])
    dma('sp', prm[:, 12:13], hgn[:, :], [], ["prm12"])
    dma('sp', prm[:, 13:14], flag[:, :], [], ["prm13"])
    dma('sp', snk_t[:], snk[:, :], [], ["snk"])
    if do_sample:
        dma('sp', ksn[:, 0:127, :], ck[:, 1:128, :], [], ["ksn_lo"])
        dma('sp', vsn[:, 0:127, :], cv[:, 1:128, :], [], ["vsn_lo"])
    ve('pool', 'memset', [], ["ones_b"], ones_b[:], 1.0)
    ve('pool', 'memset', [], ["ones_f"], ones_f[:], 1.0)
    ve('pool', 'memset', [], ["zeros_f"], zeros_f[:], 0.0)
    ve('pool', 'memset', [], ["mhalf"], mhalf[:], -0.5)
    for h in range(2):
        ve('pool', 'memset', [], [f"oz{h}"], oz[h][:], 0.0)
        ve('pool', 'memset', [f"oz{h}"], [f"oz{h}"], oz[h][:, h * 64:(h + 1) * 64], 1.0)
    for i in range(2):
        for h in range(2):
            ve('pool', 'memset', [], [f"vz{i}_{h}"], vz[i][h], 0.0)
    for h in range(2):
        ve('pool', 'memset', [], [f"vzm{h}"], vzm[h], 0.0)
    S.add('pool', lambda e: e.affine_select(ident_f[:], ones_f[:], [[-1, 128]], ALU.is_equal, 0.0,
                                            base=0, channel_multiplier=1), ["ones_f"], ["ident_f"])
    ve('pool', 'tensor_copy', ["ident_f"], ["ident_b"], ident_b[:], ident_f[:])
    S.add('pool', lambda e: e.affine_select(m_cur[:], ones_b[:], [[0, 4], [1, 128]], ALU.is_ge, 0.0,
                                            base=0, channel_multiplier=-1), ["ones_b"], ["m_cur"])
    S.add('pool', lambda e: e.affine_select(m_prev[:], ones_b[:], [[0, 4], [-1, 128]], ALU.is_gt, 0.0,
                                            base=0, channel_multiplier=1), ["ones_b"], ["m_prev"])
    ve('pool', 'tensor_scalar', ["m_prev", "prm13"], ["m_prev0"], m_prev0[:], m_prev[:], prm[:, 13:14], 1.0, ALU.mult, ALU.mult)
    S.add('pool', lambda e: e.affine_select(m_hg[:], ones_b[:], [[0, 4], [1, 128]], ALU.is_ge, 0.0,
                                            base=0, channel_multiplier=-1), ["ones_b"], ["m_hg"])
    ve('pool', 'memset', ["m_hg"], ["m_hg"], v4(m_hg[:])[0:64, :, 64:128], 0.0)
    ve('dve', 'tensor_sub', ["lbp"], ["prm0"], prm[:, 0:4], lbp_t[:, 0:4], lbp_t[:, 4:8])
    act(prm[:, 16:20], prm[:, 0:4], AF.Tanh, ["prm0"], ["prm16"], scale=0.5)
    ve('dve', 'tensor_scalar', ["prm16"], ["prm4"], prm[:, 4:8], prm[:, 16:20], -0.25, 0.25, ALU.mult, ALU.add)
    ve('dve', 'tensor_scalar', ["prm16"], ["prm0"], prm[:, 0:4], prm[:, 16:20], 0.25, 0.75, ALU.mult, ALU.add)
    ve('dve', 'tensor_scalar', ["prm16"], ["prm8"], prm[:, 8:12], prm[:, 16:20], 0.25, -0.25, ALU.mult, ALU.add)
    ve('dve', 'tensor_scalar', ["prm12"], ["prm14"], prm[:, 14:15], prm[:, 12:13], 0.5, None, ALU.mult)
    act(snk_t[:], snk_t[:], AF.Exp, ["snk"], ["snk"])
    for g in range(4):
        ve('dve', 'tensor_scalar', ["snk", "ones_f"], ["esink"], v4(esink[:])[:, g, :], ones_f[:], snk_t[:, g:g + 1], None, ALU.mult)
    ve('dve', 'memset', [], ["prm15"], prm[:, 15:16], EPS)
    ve('dve', 'memset', [], ["Sst"], Sst, 0.0)
    ve('dve', 'memset', [], ["Sbf"], Sbf, 0.0)
    ve('dve', 'memset', [], ["cp"], cp, 1.0)
    ve('dve', 'memset', [], ["t1m"], t1m[:], 1.0)

    if stop == 1:
        return finish()
    def wload(dst3, src, c0, c1, key, extra=()):
        return dma('pool', dst3[:, :, c0:c1], src.rearrange("(k p) n -> p k n", p=128)[:, :, c0:c1], [], [key], extra=extra)

    wload(WIN, w_in, OFH, OIH, "w_fh")
    wload(WIN, w_in, OIH, OGH, "w_ih")
    wload(WIN, w_in, OQ, OQH, "w_qkv")
    wload(WIN, w_in, OQH, OFH, "w_qh")
    wload(WIN, w_in, OGH, OGA, "w_gh")
    wload(WIN, w_in, OGA, OGA + 512, "w_ga0")
    wload(WIN, w_in, OGA + 512, OGB, "w_ga1")
    wload(WIN, w_in, OGB, OGB + 512, "w_gb0")
    wload(WIN, w_in, OGB + 512, DIN, "w_gb1")
    wload(WAO, w_ao, 0, D, "w_ao")
    wload(WHO, w_ho, 0, D, "w_ho")
    wload(WO, w_o, 0, D, "w_o")
    WKEYS = ["w_fh", "w_ih", "w_qkv", "w_qh", "w_gh", "w_ga0", "w_ga1", "w_gb0", "w_gb1", "w_ao", "w_ho", "w_o"]

    if stop == 2:
        return finish()
    xcnt = [0]

    class Loader:
        def __init__(self, items):
            self.items = items
            self.n_issued = 0

        def get(self, i, ahead=1):
            while self.n_issued <= min(i + ahead, len(self.items) - 1):
                j = self.n_issued
                src, n, rk = self.items[j]
                dma('sp', xt[j % 2][0:n, :], src, rk, [f"xt{j % 2}"])
                self.n_issued += 1
            return xt[i % 2], f"xt{i % 2}"

    def rstd_rows(x, xk, n, junk=None):
        jt, jk = junk or (xn, "xn")
        S.add('dve', lambda e: e.scalar_tensor_tensor(jt[0:n, :], x[0:n, :], 1.0, x[0:n, :], ALU.mult, ALU.mult,
                                                      accum_out=ss[0:n, 0:1]), [xk], [jk, "ss0"])
        ve('dve', 'tensor_scalar', ["ss0"], ["ss1"], ss[0:n, 1:2], ss[0:n, 0:1], 1.0 / D, EPS, ALU.mult, ALU.add)
        ve('pool', 'tensor_tensor', ["ss1", "mhalf"], ["ss2"], ss[0:n, 2:3], ss[0:n, 1:2], mhalf[0:n, 0:1], ALU.pow)

    def front(x, xk, n, gi, xslot=None):
        if xslot is None:
            slot = xcnt[0] % 2
            xcnt[0] += 1
            xT_t, tk = xnT[slot], f"xnT{slot}"
        else:
            xT_t, tk = xslot
        rstd_rows(x, xk, n)
        S.add('dve', lambda e: e.scalar_tensor_tensor(xn[0:n, :], x[0:n, :], ss[0:n, 2:3], gbt[0:n, :],
                                                      ALU.mult, ALU.mult), [xk, "ss2", "gbt"], ["xn"])
        for k in range(8):
            tr(T0[:, k, 0:n], xn[0:n, k * 128:(k + 1) * 128], ident_b[0:n, 0:n], ["xn", "ident_b"], ["T0"])
        act(xT_t[:, :, 0:n], T0[:, :, 0:n], AF.Copy, ["T0"], [tk])
        return xT_t, tk

    def proj_fm(bank, bkey, xT, tk, n, col0, nch, wkey):
        b3 = v4(bank[:])
        for c in range(nch):
            for k in range(8):
                mm(b3[:, c, 0:n], WIN[:, k, col0 + c * 128:col0 + (c + 1) * 128], xT[:, k, 0:n],
                   k == 0, k == 7, [tk, wkey], [bkey])
        return b3

    def proj_tm(out_ap, bkey, xT, tk, n, col0, ncol, wkey):
        for k in range(8):
            mm(out_ap, xT[:, k, 0:n], WIN[:, k, col0:col0 + ncol], k == 0, k == 7, [tk, wkey], [bkey])

    dec34 = [sb(f"dec{i}x", [128, 8]) for i in (2, 3)]
    th3 = sb("th3", [128, 512])
    ketok2 = ketok2 + [hgT[:], attT[:]]
    itok2 = itok2 + [osq[:], kvo[:].bitcast(BF16)]
    dec2 = dec2 + [d[:] for d in dec34]
    def KK(sl):
        return f"ketok{sl}" if sl < 2 else ("hgT", "attT")[sl - 2]

    def IK(sl):
        return f"itok{sl}" if sl < 2 else ("osq", "kvo_k")[sl - 2]

    def IKW(sl):
        return [IK(sl)] + (["kvo_v"] if sl == 3 else [])
    SLK = ["0", "1", "hgT", "attT"]
    SLI = ["0", "1", "osq", "kvo"]
    BS = [dict(th=(th[:], "th"), f=(fk[:, 0, :], "f"), k=(fk[:, 1, :], "k"), cp=(cp, "cp"), rc=(rc, "rc"),
               keT=(keT, "keT"), tsl=0, ib=(B[2], "B2"), fb=(B[0], "B0")),
          dict(th=(sgh[:], "sgh"), f=(mse[:], "mse"), k=(rden[:], "rden"), cp=(t1m[:], "t1m"), rc=(t2m[:], "t2m"),
               keT=(qT[:], "qT"), tsl=4, ib=(B[3], "B3"), fb=(B[1], "B1"))]

    def hg_gates_a(xT, tk, n, bs=None):
        bs = bs or BS[0]
        b3 = proj_fm(bs['fb'][0], bs['fb'][1], xT, tk, n, OFH, 4, "w_fh")
        act(v4(bs['th'][0])[:, :, 0:n], b3[:, :, 0:n], AF.Tanh, [bs['fb'][1]], [bs['th'][1]], scale=0.5)

    def hg_gates_b(n, bs=None):
        bs = bs or BS[0]
        t3 = v4(bs['th'][0])
        tkey = bs['th'][1]
        f3 = v4(bs['f'][0])
        k3 = v4(bs['k'][0])
        for h in range(4):
            ve('dve', 'tensor_scalar', [tkey, "prm0", "prm4"], [bs['f'][1]], f3[:, h, 0:n], t3[:, h, 0:n],
               prm[:, 4 + h:5 + h], prm[:, h:h + 1], ALU.mult, ALU.add)
            ve('dve', 'tensor_scalar', [tkey, "prm8", "prm4"], [bs['k'][1]], k3[:, h, 0:n], t3[:, h, 0:n],
               prm[:, 8 + h:9 + h], prm[:, 4 + h:5 + h], ALU.mult, ALU.add)
        return f3, k3

    def hg_gates(xT, tk, n):
        hg_gates_a(xT, tk, n)
        return hg_gates_b(n)

    hslot = [0]

    def hg_prep_a(xT, tk, gates_done=False, bs=None, nsl=2):
        bs = bs or BS[0]
        sl = hslot[0] % nsl
        hslot[0] += 1
        dec = dec2[sl]
        if not gates_done:
            hg_gates_a(xT, tk, 128, bs)
        f3, k3 = hg_gates_b(128, bs)
        cpt, cpk = bs['cp']
        rct, rck = bs['rc']
        ket, kek = bs['keT']
        c3 = v4(cpt)
        for h in range(4):
            for c in range(2):
                S.add('dve', lambda e, h=h, c=c, c3=c3, f3=f3: e.tensor_tensor_scan(
                    c3[:, h, c * 64:(c + 1) * 64], f3[:, h, c * 64:(c + 1) * 64], zeros_f[:, 0:64], 1.0,
                    ALU.mult, ALU.add), [bs['f'][1], "zeros_f"], [cpk])
        ve('dve', 'reciprocal', [cpk], [rck], rct, cpt)
        ve('dve', 'tensor_tensor', [bs['k'][1], rck], [kek], ket, bs['k'][0], rct, ALU.mult)
        ve('dve', 'tensor_copy', [cpk], [f"dec{sl}"], dec.rearrange("p (h c) -> p h c", h=4),
           cpt.rearrange("p (h c t) -> p h c t", h=4, c=2)[:, :, :, 63])
        return sl

    def hg_prep_a_rev(xT, tk, bs, nsl=4):
        sl = hslot[0] % nsl
        hslot[0] += 1
        dec = dec2[sl]
        f3, k3 = hg_gates_b(128, bs)
        cpt, cpk = bs['cp']
        ket, kek = bs['keT']
        c3 = v4(cpt)
        for h in range(4):
            for c in range(2):
                S.add('dve', lambda e, h=h, c=c, c3=c3, f3=f3: e.tensor_tensor_scan(
                    c3[:, h, c * 64 + 1:(c + 1) * 64], f3[:, h, c * 64:(c + 1) * 64 - 1], zeros_f[:, 0:63], 1.0,
                    ALU.mult, ALU.add), [bs['f'][1], "zeros_f"], [cpk])
        ve('dve', 'tensor_tensor', [bs['k'][1], cpk], [kek], ket, bs['k'][0], cpt, ALU.mult)
        ve('dve', 'tensor_tensor', [cpk, bs['f'][1]], [f"dec{sl}"], dec.rearrange("p (h c) -> p h c", h=4),
           cpt.rearrange("p (h c t) -> p h c t", h=4, c=2)[:, :, :, 63],
           bs['f'][0].rearrange("p (h c t) -> p h c t", h=4, c=2)[:, :, :, 63], ALU.mult)
        return sl

    def hg_state_update_rev(sl, c, ubank, ukey):
        ketok, itok, dec = ketok2[sl], itok2[sl], dec2[sl]
        u3 = v4(ubank[:])
        for h in range(4):
            mm(u3[:, h, :], v4(ketok)[c * 64:(c + 1) * 64, h, :], v4(itok)[c * 64:(c + 1) * 64, h, :],
               True, True, [KK(sl)] + IKW(sl), [ukey])
        for h in range(4):
            dcol = dec[:, h * 2 + c:h * 2 + c + 1]
            S.add('dve', lambda e, h=h, dcol=dcol, u3=u3: e.scalar_tensor_tensor(
                v4(Sst)[:, h, :], v4(Sst)[:, h, :], dcol, u3[:, h, :], ALU.mult, ALU.add),
                ["Sst", ukey, f"dec{sl}"], ["Sst"])

    def hg_prep_i(sl, xT, tk, bs=None):
        bs = bs or BS[0]
        proj_tm(bs['ib'][0][:, :], bs['ib'][1], xT, tk, 128, OIH, 512, "w_ih")
        act(itok2[sl], bs['ib'][0][:], AF.Copy, [bs['ib'][1]], IKW(sl))

    def hg_prep_t(sl, bs=None):
        bs = bs or BS[0]
        o = bs['tsl']
        for h in range(4):
            tr(T1[:, o + h, :], v4(bs['keT'][0])[:, h, :], ident_b[:], [bs['keT'][1], "ident_b"], ["T1"])
        act(v4(ketok2[sl]), T1[:, o:o + 4, :], AF.Copy, ["T1"], [KK(sl)])

    def hg_chunk_prep(xT, tk):
        sl = hg_prep_a(xT, tk)
        hg_prep_t(sl)
        hg_prep_i(sl, xT, tk)
        return sl

    def hg_state_update(sl, c, ubank, ukey):
        ketok, itok, dec = ketok2[sl], itok2[sl], dec2[sl]
        u3 = v4(ubank[:])
        for h in range(4):
            mm(u3[:, h, :], v4(ketok)[c * 64:(c + 1) * 64, h, :], v4(itok)[c * 64:(c + 1) * 64, h, :],
               True, True, [KK(sl)] + IKW(sl), [ukey])
        ve('dve', 'tensor_tensor', ["Sst", ukey], ["Ttmp"], Ttmp, Sst, ubank[:], ALU.add)
        for h in range(4):
            dcol = dec[:, h * 2 + c:h * 2 + c + 1]
            act(v4(Sst)[:, h, :], v4(Ttmp)[:, h, :], AF.Identity, ["Ttmp", f"dec{sl}"], ["Sst"], scale=dcol)
            ve('pool', 'tensor_scalar', ["Ttmp", f"dec{sl}"], ["Sbf"], v4(Sbf)[:, h, :], v4(Ttmp)[:, h, :], dcol, 1.0, ALU.mult, ALU.mult)

    def kv_tile(xT, tk, n, kT_t, kT_key, vz_t, vz_key, want_k_tok, out_k, out_v):
        for k in range(8):
            mm(B[5][:, 0:n], WIN[:, k, OK_:OK_ + 128], xT[:, k, 0:n], k == 0, k == 7, [tk, "w_qkv"], ["B5"])
        for k in range(8):
            mm(B[5][0:n, 128:256], xT[:, k, 0:n], WIN[:, k, OV:OV + 128], k == 0, k == 7, [tk, "w_qkv"], ["B5"])
        if want_k_tok:
            for k in range(8):
                mm(B[5][0:n, 256:384], xT[:, k, 0:n], WIN[:, k, OK_:OK_ + 128], k == 0, k == 7, [tk, "w_qkv"], ["B5"])
        if kT_t is not None:
            act(kT_t[:, 0:n], B[5][:, 0:n], AF.Copy, ["B5"], [kT_key])
            for h in range(2):
                act(vz_t[h][0:n, h * 64:(h + 1) * 64], B[5][0:n, 128 + h * 64:128 + (h + 1) * 64], AF.Copy,
                    ["B5"], [vz_key + str(h)])
        ids = []
        if out_v is not None:
            ve('dve', 'tensor_copy', ["B5"], ["kvo_v"], kvo[0:n, 128:256], B[5][0:n, 128:256])
            ids.append(dma('sp', out_v, kvo[0:n, 128:256], ["kvo_v"], ["out_v"]))
        if want_k_tok:
            ve('dve', 'tensor_copy', ["B5"], ["kvo_k"], kvo[0:n, 0:128], B[5][0:n, 256:384])
            ids.append(dma('sp', out_k, kvo[0:n, 0:128], ["kvo_k"], ["out_k"]))
        return ids

    def hg_out(ob, okey, n, xT, tk):
        o3 = v4(ob[:])
        g3 = proj_fm(B[3], "B3", xT, tk, n, OGH, 4, "w_gh")
        act(v4(th[:])[:, :, 0:n], g3[:, :, 0:n], AF.Tanh, ["B3"], ["th"], scale=0.5)
        S.add('dve', lambda e: e.scalar_tensor_tensor(v4(sgh[:])[:, :, 0:n], v4(th[:])[:, :, 0:n], 1.0, g3[:, :, 0:n],
                                                      ALU.add, ALU.mult), ["th", "B3"], ["sgh"])
        act(v4(osq[:])[:, :, 0:n], o3[:, :, 0:n], AF.Square, [okey], ["osq"])
        s3 = v4(B[0][:])
        if n == 128:
            mm(B[0][:, :], ones_b[:, 0:128], osq[:, :], True, True, ["osq", "ones_b"], ["B0"])
        else:
            for hh in range(4):
                mm(s3[:, hh, 0:n], ones_b[:, 0:128], v4(osq[:])[:, hh, 0:n], True, True, ["osq", "ones_b"], ["B0"])
        act(v4(mse[:])[:, :, 0:n], s3[:, :, 0:n], AF.Sqrt, ["B0", "prm15"], ["mse"], scale=1.0 / 128, bias=prm[:, 15:16])
        ve('dve', 'reciprocal', ["mse"], ["mse"], v4(mse[:])[:, :, 0:n], v4(mse[:])[:, :, 0:n])
        ve('dve', 'tensor_tensor', [okey, "mse"], ["mse"], v4(mse[:])[:, :, 0:n], o3[:, :, 0:n], v4(mse[:])[:, :, 0:n], ALU.mult)
        S.add('dve', lambda e: e.scalar_tensor_tensor(v4(hgT[:])[:, :, 0:n], v4(mse[:])[:, :, 0:n], prm[:, 14:15],
                                                      v4(sgh[:])[:, :, 0:n], ALU.mult, ALU.mult),
              ["mse", "sgh", "prm14"], ["hgT"])

    def gates_proj(xT, tk, n):
        for (wk0, wk1, col0, tht, thk, ba, bb, ka, kb) in (
                ("w_ga0", "w_ga1", OGA, tha, "tha", B[2], B[3], "B2", "B3"),
                ("w_gb0", "w_gb1", OGB, thb, "thb", B[4], B[5], "B4", "B5")):
            proj_fm(ba, ka, xT, tk, n, col0, 4, wk0)
            proj_fm(bb, kb, xT, tk, n, col0 + 512, 4, wk1)
            act(v8(tht[:])[:, 0:4, 0:n], v4(ba[:])[:, :, 0:n], AF.Tanh, [ka], [thk], scale=0.5)
            act(v8(tht[:])[:, 4:8, 0:n], v4(bb[:])[:, :, 0:n], AF.Tanh, [kb], [thk], scale=0.5)

    def merge_rest(x, xk, n):
        for half in range(2):
            for (W3, wkey, srcT, skey, tht, thk, tm, tmk, bank, bkey) in (
                    (WAO, "w_ao", attT, "attT", tha, "tha", t1m, "t1m", B[2 + half], f"B{2 + half}"),
                    (WHO, "w_ho", hgT, "hgT", thb, "thb", t2m, "t2m", B[4 + half], f"B{4 + half}")):
                o3 = v4(bank[:])
                for c in range(4):
                    m = half * 4 + c
                    for k in range(4):
                        mm(o3[:, c, 0:n], W3[:, k, m * 128:(m + 1) * 128], v4(srcT[:])[:, k, 0:n], k == 0, k == 3,
                           [skey, wkey], [bkey])
                S.add('dve', lambda e, half=half, o3=o3, tht=tht, tm=tm: e.scalar_tensor_tensor(
                    v4(tm[:])[:, :, 0:n], v8(tht[:])[:, half * 4:half * 4 + 4, 0:n], 1.0,
                    o3[:, :, 0:n], ALU.add, ALU.mult), [thk, bkey], [tmk])
            ve('pool', 'tensor_tensor', ["t1m", "t2m"], ["mixT"], v8(mixT[:])[:, half * 4:half * 4 + 4, 0:n],
               v4(t1m[:])[:, :, 0:n], v4(t2m[:])[:, :, 0:n], ALU.add)
        for nb, (bank, bkey) in enumerate(((B[0], "B0"), (B[1], "B1"))):
            for k in range(8):
                mm(bank[0:n, :], v8(mixT[:])[:, k, 0:n], WO[:, k, nb * 512:(nb + 1) * 512], k == 0, k == 7,
                   ["mixT", "w_o"], [bkey])
            S.add('dve', lambda e, bank=bank, nb=nb: e.scalar_tensor_tensor(
                x[0:n, nb * 512:(nb + 1) * 512], bank[0:n, :], 0.5, x[0:n, nb * 512:(nb + 1) * 512],
                ALU.mult, ALU.add), [bkey, xk], [xk])

    itemsA = [(xp[t * 128:(t + 1) * 128, :], 128, []) for t in range(npre)]
    itemsA += [(xmeta[:, :], 16, []), (xhalo[:, :], 128, [])]
    itemsA += [(xmain[t * 128:(t + 1) * 128, :], 128, []) for t in range(nt)]
    itemsA += [(xs[:, :], NS, [])]
    LA = Loader(itemsA)
    XS = [(xnT[0], "xnT0"), (xnT[1], "xnT1"),
          (tha[:].rearrange("p (k t) -> p k t", k=8), "tha"), (thb[:].rearrange("p (k t) -> p k t", k=8), "thb")]
    THB = [[(th[:], "th"), (sgh[:], "sgh")], [(mixT[:].bitcast(F32), "mixT"), (th3[:], "th3")]]

    def BSG(gidx, i):
        return dict(BS[i], th=THB[gidx % 2][i])

    def pre_a(t, bs):
        x, xk = LA.get(t)
        xT, tk = front(x, xk, 128, 0, xslot=XS[t % 4])
        hg_gates_a(xT, tk, 128, bs)
        return xT, tk

    def pre_b(xT, tk, bs):
        sl = hg_prep_a_rev(xT, tk, bs, nsl=4)
        hg_prep_t(sl, bs)
        hg_prep_i(sl, xT, tk, bs)
        return sl

    def rec_pre_b_group(alist, gidx):
        if len(alist) == 1:
            S.rec = []
            sl0 = pre_b(*alist[0], BSG(gidx, 0))
            l = S.rec
            S.rec = None
            return [sl0], l
        S.rec = []
        sl0 = pre_b(*alist[0], BSG(gidx, 0))
        l0 = S.rec
        S.rec = []
        sl1 = pre_b(*alist[1], BSG(gidx, 1))
        l1 = S.rec
        S.rec = None
        out = []
        i0 = i1 = 0
        while i0 < len(l0) or i1 < len(l1):
            if i0 < len(l0):
                out.append(l0[i0])
                i0 += 1
            if i1 < len(l1):
                out.append(l1[i1])
                i1 += 1
        return [sl0, sl1], out

    def rec_s2(sls):
        S.rec = []
        for sl_cur in sls:
            for c in range(2):
                ub, uk = (B[4], "B4") if c == 0 else (B[5], "B5")
                hg_state_update_rev(sl_cur, c, ub, uk)
        l = S.rec
        S.rec = None
        return l

    def emit_merged(la, lb):
        i1 = i2 = 0
        n1, n2 = len(la), len(lb)
        while i1 < n1 or i2 < n2:
            if i2 >= n2 or (i1 < n1 and i1 * n2 <= i2 * n1):
                S.add(*la[i1])
                i1 += 1
            else:
                S.add(*lb[i2])
                i2 += 1

    pgroups = [list(range(g, min(g + 2, npre))) for g in range(0, npre, 2)]

    def rec_pre_a_group(gidx):
        S.rec = []
        r = [pre_a(tt, BSG(gidx, i)) for i, tt in enumerate(pgroups[gidx])]
        l = S.rec
        S.rec = None
        return r, l

    def emit_merged3(ls):
        ls = [l for l in ls if l]
        idx = [0] * len(ls)
        while True:
            best, bi = None, -1
            for i, l in enumerate(ls):
                if idx[i] < len(l):
                    frac = idx[i] / len(l)
                    if best is None or frac < best:
                        best, bi = frac, i
            if bi < 0:
                break
            S.add(*ls[bi][idx[bi]])
            idx[bi] += 1

    G = len(pgroups)
    a_res = {}
    pend_sl = {}
    if G > 0:
        a_res[0], la0 = rec_pre_a_group(0)
        emit_merged3([la0])
        pend_sl[0], lb0 = rec_pre_b_group(a_res[0], 0)
        emit_merged3([lb0])
    if G > 1:
        a_res[1], la1 = rec_pre_a_group(1)
        emit_merged3([la1])
    for gi in range(G):
        ls2 = rec_s2(pend_sl[gi])
        lb_, la_ = [], []
        if gi + 1 < G:
            pend_sl[gi + 1], lb_ = rec_pre_b_group(a_res[gi + 1], gi + 1)
        if gi + 2 < G:
            a_res[gi + 2], la_ = rec_pre_a_group(gi + 2)
        emit_merged3([ls2, lb_, la_])
        t = gi
        if do_ffn and 2 <= t < 10:
            j = t - 2
            if j < 4:
                dma('pool', wupb[j * 256:(j + 1) * 256, :], w_up[j * 256:(j + 1) * 256, :], [], [f"wupb{j}"])
            else:
                j -= 4
                dma('pool', wdnb[j * 1024:(j + 1) * 1024, :], w_dn[j * 1024:(j + 1) * 1024, :], [], [f"wdnb{j}"])

    ve('pool', 'tensor_copy', ["Sst"], ["Sbf"], Sbf, Sst)
    fenceA_dma = []
    x, xk = LA.get(npre)
    xT, tk = front(x, xk, 16, 0)
    fenceA_dma += kv_tile(xT, tk, 16, kTm, "kTm", vzm, "vzm", True, mk[:, :], mv[:, :])
    if stop == 3:
        return finish()
    x, xk = LA.get(npre + 1)
    xT, tk = front(x, xk, 128, 0)
    kv_tile(xT, tk, 128, kT[1], "kT1", vz[1], "vz1_", False, None, None)
    if stop == 4:
        return finish()

    fS = []
    pre_done = [False]

    def sample_prefetch():
        if pre_done[0] or not do_sample:
            return
        pre_done[0] = True
        fS.append(dma('sp', kst[0:127, :, :], ck[0:8, 1:128, :].rearrange("b r c -> r b c"), [], ["kst"]))
        fS.append(dma('sp', vst[0:127, :, :], cv[0:8, 1:128, :].rearrange("b r c -> r b c"), [], ["vst"]))
        fS.append(dma('sp', mst[:, 0, :], cmk[0:8, :, :].rearrange("b r c -> (b r) c"), [], ["mst"]))
        fS.append(dma('sp', mst[:, 1, :], cmv[0:8, :, :].rearrange("b r c -> (b r) c"), [], ["mst"]))
        for g in range(2):
            fS.append(dma('sp', Ssm2[g][:, :, :], sth[g * 2:(g + 1) * 2].rearrange("b h d v -> d (b h) v"), [], [f"Ssm{g}"]))

    for t in range(nt):
        if t == nt - 1:
            sample_prefetch()
        cur = t % 2
        prv = 1 - cur
        x, xk = LA.get(npre + 2 + t)
        if t == 0:
            xT, tk = front(x, xk, 128, 0)
        else:
            xT, tk = nxt_front
        last = (t == nt - 1)
        sl = hg_prep_a(xT, tk)
        fenceA_dma += kv_tile(xT, tk, 128, kT[cur], f"kT{cur}", vz[cur], f"vz{cur}_", last,
                              kp[:, :] if last else None, vp[:, :] if last else None)
        hg_prep_i(sl, xT, tk)
        itok = itok2[sl]
        gates_proj(xT, tk, 128)
        proj_fm(B[1], "B1", xT, tk, 128, OQ, 4, "w_qkv")
        act(qT[:], B[1][:], AF.Copy, ["B1"], ["qT"])
        q3s = v4(qT[:])
        groups = [(kTm, "kTm", vzm, "vzm", 16, None, None),
                  (kT[prv], f"kT{prv}", vz[prv], f"vz{prv}_", 128, m_prev0 if t == 0 else m_prev,
                   "m_prev0" if t == 0 else "m_prev"),
                  (kT[cur], f"kT{cur}", vz[cur], f"vz{cur}_", 128, m_cur, "m_cur")]

        def score(i):
            h, gi = divmod(i, 3)
            kt, ktk, vzt, vzk, nk, mask, mkey = groups[gi]
            sbk, sbkey = (B[0], "B0") if (i % 2 == 0) else (B[5], "B5")
            mm(sbk[0:nk, :], kt[h * 64:(h + 1) * 64, 0:nk], q3s[h * 64:(h + 1) * 64, :, :], True, True,
               [ktk, "qT"], [sbkey])

        score(0)
        for i in range(6):
            h, gi = divmod(i, 3)
            kt, ktk, vzt, vzk, nk, mask, mkey = groups[gi]
            sbk, sbkey = (B[0], "B0") if (i % 2 == 0) else (B[5], "B5")
            if i + 1 < 6:
                score(i + 1)
            P = PT[i % 3]
            pkey = f"PT{i % 3}"
            act(P[0:nk, :], sbk[0:nk, :], AF.Exp, [sbkey], [pkey], scale=0.125)
            if mask is not None:
                ve('dve', 'tensor_tensor', [pkey, mkey], [pkey], P[0:nk, :], P[0:nk, :], mask[0:nk, :], ALU.mult)
            mm(B[2][:, :], vzt[h][0:nk, :], P[0:nk, :], i == 0, i == 5, [vzk + str(h), pkey], ["B2"])
            mm(B[3][:, :], oz[h][0:nk, :], P[0:nk, :], i == 0, i == 5, [f"oz{h}", pkey], ["B3"])
        ve('dve', 'tensor_tensor', ["B3", "esink"], ["rden"], rden[:], B[3][:], esink[:], ALU.add)
        ve('dve', 'reciprocal', ["rden"], ["rden"], rden[:], rden[:])
        ve('dve', 'tensor_tensor', ["B2", "rden"], ["attT"], attT[:], B[2][:], rden[:], ALU.mult)
        hg_prep_t(sl)
        q3 = proj_fm(B[1], "B1", xT, tk, 128, OQH, 4, "w_qh")
        S.add('dve', lambda e, q3=q3: e.scalar_tensor_tensor(v4(qeT), q3, 128 ** -0.5, v4(cp),
                                                            ALU.mult, ALU.mult), ["B1", "cp"], ["qeT"])
        a3 = v4(B[4][:])
        for h in range(4):
            mm(a3[:, h, :], v4(keT)[:, h, :], v4(qeT)[:, h, :], True, True, ["keT", "qeT"], ["B4"])
        ve('dve', 'tensor_tensor', ["B4", "m_hg"], ["AT"], AT, B[4][:], m_hg[:], ALU.mult)
        o3 = v4(B[1][:])
        for c in range(2):
            for h in range(4):
                osl = o3[:, h, c * 64:(c + 1) * 64]
                mm(osl, v4(Sbf)[:, h, :], v4(qeT)[:, h, c * 64:(c + 1) * 64], True, False, ["Sbf", "qeT"], ["B1"])
                mm(osl, v4(itok)[:, h, :], v4(AT)[:, h, c * 64:(c + 1) * 64], False, True, [IK(sl), "AT"], ["B1"])
            hg_state_update(sl, c, B[5], "B5")
        hg_out(B[1], "B1", 128, xT, tk)
        if t + 1 < nt:
            nx, nxk = LA.get(npre + 2 + t + 1, ahead=0)
            nxt_front = front(nx, nxk, 128, 0)
        merge_rest(x, xk, 128)
        dma('sp', h1s[t * 128:(t + 1) * 128, :], x[:, :], [xk], [f"h1s{t}"])
    fenceA_dma.append(dma('sp', spo.rearrange("h d v -> d h v"), v4(Sst), ["Sst"], ["spo"]))

    x_s = xk_s = None
    if do_sample:
        S.fence(fenceA_dma)
        n = NS
        x_s, xk_s = LA.get(npre + 2 + nt)
        xT, tk = front(x_s, xk_s, n, 0)
        for k in range(8):
            mm(B[5][0:n, 0:128], xT[:, k, 0:n], WIN[:, k, OK_:OK_ + 128], k == 0, k == 7, [tk, "w_qkv"], ["B5"])
        for k in range(8):
            mm(B[5][0:n, 128:256], xT[:, k, 0:n], WIN[:, k, OV:OV + 128], k == 0, k == 7, [tk, "w_qkv"], ["B5"])
        ve('dve', 'tensor_copy', ["B5"], ["kvo_k"], kvo[0:n, 0:128], B[5][0:n, 0:128])
        ve('dve', 'tensor_copy', ["B5"], ["kvo_v"], kvo[0:n, 128:256], B[5][0:n, 128:256])
        dma('sp', ksn[:, 127, :], kvo[0:n, 0:128], ["kvo_k"], ["ksn127"])
        dma('sp', vsn[:, 127, :], kvo[0:n, 128:256], ["kvo_v"], ["vsn127"])
        qb3 = proj_fm(B[4], "B4", xT, tk, n, OQ, 4, "w_qkv")
        act(v4(qT[:])[:, :, 0:n], qb3[:, :, 0:n], AF.Copy, ["B4"], ["qT"])
        S.add('pool', lambda e: e.affine_select(mblk, ones_b[:, 0:64], [[0, 4], [-16, 16]], ALU.is_ge, 0.0,
                                                base=0, channel_multiplier=1), ["ones_b"], ["mblk"])
        S.add('pool', lambda e: e.affine_select(mblk, mblk, [[0, 4], [16, 16]], ALU.is_ge, 0.0,
                                                base=15, channel_multiplier=-1), ["mblk"], ["mblk"])
        mb3 = mblk.rearrange("p (g b) -> p g b", g=4)[:, :, 0:8]
        pm4 = pm.rearrange("p h (g b) -> p h g b", g=4)
        n3 = v4(B[2][:])
        d3 = v4(B[3][:])
        for h in range(2):
            ve('pool', 'memset', [], [f"wvz{h}"], wvz[h], 0.0)
            ve('pool', 'memset', [], [f"mvz{h}"], mvz[h], 0.0)
        for hb in range(2):
            b0 = hb * 8
            if hb == 0:
                sample_prefetch()
            else:
                fS.append(dma('sp', kst[0:127, :, :], ck[b0:b0 + 8, 1:128, :].rearrange("b r c -> r b c"), [], ["kst"]))
                fS.append(dma('sp', vst[0:127, :, :], cv[b0:b0 + 8, 1:128, :].rearrange("b r c -> r b c"), [], ["vst"]))
                fS.append(dma('sp', mst[:, 0, :], cmk[b0:b0 + 8, :, :].rearrange("b r c -> (b r) c"), [], ["mst"]))
                fS.append(dma('sp', mst[:, 1, :], cmv[b0:b0 + 8, :, :].rearrange("b r c -> (b r) c"), [], ["mst"]))
            fS.append(dma('sp', kst[127:128, :, :], ksn[b0:b0 + 8, 127:128, :].rearrange("b r c -> r b c"), ["ksn127"], ["kst"]))
            fS.append(dma('sp', vst[127:128, :, :], vsn[b0:b0 + 8, 127:128, :].rearrange("b r c -> r b c"), ["vsn127"], ["vst"]))
            ve('dve', 'tensor_copy', ["kst"], ["wk"], wk, kst)
            ve('pool', 'tensor_copy', ["mst"], ["mkb"], mkb[:, hb, :], mst[:, 0, :])
            for h in range(2):
                ve('pool', 'tensor_copy', ["vst", f"wvz{h}"], [f"wvz{h}"], wvz[h][:, :, h * 64:(h + 1) * 64], vst[:, :, h * 64:(h + 1) * 64])
                ve('pool', 'tensor_copy', ["mst", f"mvz{h}"], [f"mvz{h}"], mvz[h][:, hb, h * 64:(h + 1) * 64], mst[:, 1, h * 64:(h + 1) * 64])
            for j in range(8):
                tr(T0[:, j, :], wk[:, j, :], ident_b[:], ["wk", "ident_b"], ["T0"])
            act(wkT[:, :, :], T0[:, :, :], AF.Copy, ["T0"], ["wkT"])
            tr(T1[:, 0, :], mkb[:, hb, :], ident_b[:], ["mkb", "ident_b"], ["T1"])
            act(mkT[:, hb, :], T1[:, 0, :], AF.Copy, ["T1"], ["mkT"])
            sbank = ((B[0], "B0"), (B[4], "B4"))
            mbank = ((B[1], "B1"), (B[5], "B5"))
            for j in range(8):
                b = b0 + j
                for h in range(2):
                    mm(sbank[h][0][:, j * 4:j * 4 + 4], wkT[h * 64:(h + 1) * 64, j, :], v4(qT[:])[h * 64:(h + 1) * 64, :, b], True, True,
                       ["wkT", "qT"], [sbank[h][1]])
            for h in range(2):
                m3 = mbank[h][0][:, 0:64].rearrange("p (g b) -> p g b", g=4)
                for g in range(4):
                    mm(m3[:, g, 0:8], mkT[h * 64:(h + 1) * 64, hb, :], v4(qT[:])[h * 64:(h + 1) * 64, g, b0:b0 + 8],
                       True, True, ["mkT", "qT"], [mbank[h][1]])
            for h in range(2):
                act(pw[:, h * 32:(h + 1) * 32], sbank[h][0][:, 0:32], AF.Exp, [sbank[h][1]], ["pw"], scale=0.125)
                m3 = mbank[h][0][:, 0:64].rearrange("p (g b) -> p g b", g=4)
                act(pm4[:, h, :, 0:8], m3[:, :, 0:8], AF.Exp, [mbank[h][1]], ["pm"], scale=0.125)
                ve('pool', 'tensor_tensor', ["pm", "mblk"], ["pm"], pm4[:, h, :, 0:8], pm4[:, h, :, 0:8], mb3, ALU.mult)
            for h in range(2):
                for g in range(4):
                    st0 = (hb == 0 and h == 0 and g == 0)
                    mm(n3[:, g, b0:b0 + 8], mvz[h][:, hb, :], pm4[:, h, g, 0:8], st0, False, [f"mvz{h}", "pm"], ["B2"], skip=True)
                    mm(d3[:, g, b0:b0 + 8], oz[h][:, :], pm4[:, h, g, 0:8], st0, False, [f"oz{h}", "pm"], ["B3"], skip=True)
            for j in range(8):
                b = b0 + j
                for h in range(2):
                    col = h * 32 + j * 4
                    lastw = (hb == 1 and j == 7 and h == 1)
                    mm(n3[:, :, b], wvz[h][:, j, :], pw[:, col:col + 4], False, lastw, [f"wvz{h}", "pw"], ["B2"], skip=True)
                    mm(d3[:, :, b], oz[h][:, :], pw[:, col:col + 4], False, lastw, [f"oz{h}", "pw"], ["B3"], skip=True)
        ve('dve', 'tensor_tensor', ["B3", "esink"], ["rden"], v4(rden[:])[:, :, 0:n], d3[:, :, 0:n], v4(esink[:])[:, :, 0:n], ALU.add)
        ve('dve', 'reciprocal', ["rden"], ["rden"], v4(rden[:])[:, :, 0:n], v4(rden[:])[:, :, 0:n])
        ve('dve', 'tensor_tensor', ["B2", "rden"], ["attT"], v4(attT[:])[:, :, 0:n], n3[:, :, 0:n], v4(rden[:])[:, :, 0:n], ALU.mult)
        f3, k3 = hg_gates(xT, tk, n)
        qh3 = proj_fm(B[1], "B1", xT, tk, n, OQH, 4, "w_qh")
        fs4 = fs.rearrange("p a (c t) -> p a c t", c=4)
        ve('dve', 'tensor_copy', ["f"], ["fs"], fs4[:, 0, :, :], f3[:, :, 0:n])
        ve('dve', 'tensor_copy', ["k"], ["fs"], fs4[:, 1, :, :], k3[:, :, 0:n])
        ve('dve', 'tensor_scalar', ["B1"], ["fs"], fs4[:, 2, :, :], qh3[:, :, 0:n], 128 ** -0.5, None, ALU.mult)
        proj_tm(B[4][0:n, :], "B4", xT, tk, n, OIH, 512, "w_ih")
        ve('dve', 'tensor_copy', ["B4"], ["isf"], isf, B[4][0:n, :])
        os3 = v4(B[5][:])
        iblk2 = [iblk, iblk_b]

        def bcast(bb):
            ibl, iblk_k = iblk2[bb % 2], f"iblk{bb % 2}"
            ve('dve', 'tensor_scalar', ["isf", "ident_f"], [iblk_k], ibl, isf, ident_f[0:n, bb:bb + 1], None, ALU.mult)
            ib_, ibk_ = (B[0], "B0") if (bb % 2 == 0) else (B[2], "B2")
            mm(ib_[:, :], ones_f[0:n, 0:128], ibl, True, True, [iblk_k, "ones_f"], [ibk_])

        bcast(0)
        for g in range(8):
            Ssm = Ssm2[g % 2]
            sk = f"Ssm{g % 2}"
            if g >= 2:
                fS.append(dma('sp', Ssm[:, :, :], sth[g * 2:(g + 1) * 2].rearrange("b h d v -> d (b h) v"), [], [sk]))
            for j in range(2):
                b = g * 2 + j
                for h in range(4):
                    act(Ssm[:, j * 4 + h, :], Ssm[:, j * 4 + h, :], AF.Identity, [sk, "fs"], [sk], scale=fs4[:, 0, h, b:b + 1])
            for j in range(2):
                b = g * 2 + j
                if b + 1 < n:
                    bcast(b + 1)
                ib, ibk = (B[0], "B0") if (b % 2 == 0) else (B[2], "B2")
                for h in range(4):
                    S.add('dve', lambda e, Ssm=Ssm, ib=ib, j=j, h=h, b=b: e.scalar_tensor_tensor(
                        Ssm[:, j * 4 + h, :], ib[:, h * 128:(h + 1) * 128], fs4[:, 1, h, b:b + 1], Ssm[:, j * 4 + h, :],
                        ALU.mult, ALU.add), [sk, "fs", ibk], [sk])
                    mm(os3[:, h, b:b + 1], Ssm[:, j * 4 + h, :], fs4[:, 2, h, b:b + 1], True, True, [sk, "fs"], ["B5"])
            fS.append(dma('sp', ssn[g * 2:(g + 1) * 2].rearrange("b h d v -> d (b h) v"), Ssm[:, :, :], [sk], [f"ssn{g}"]))
        hg_out(B[5], "B5", n, xT, tk)
        gates_proj(xT, tk, n)
        merge_rest(x_s, xk_s, n)

    if do_ffn:
        S.fence(fS)
        ext = []
        for kx in WKEYS:
            ext += S.readers(kx)
        dma('sp', gf, gains_f[:, :], [], ["gf"])
        dma('sp', gbt[:, :], gains_b[:, :], [], ["gbt"])
        for j in range(8):
            dma('sp', WUP[:, :, j * 512:(j + 1) * 512], wupb.rearrange("(k p) n -> p k n", p=128)[:, :, j * 512:(j + 1) * 512],
                [f"wupb{i}" for i in range(4)], [f"w_up{j}"], extra=ext)
        for j in range(8):
            dma('sp', WDN[:, j * 4:(j + 1) * 4, :], wdnb.rearrange("(k p) n -> p k n", p=128)[:, j * 4:(j + 1) * 4, :],
                [f"wdnb{j // 2}"], [f"w_dn{j}"], extra=ext)

        def ffn_up(xT, tk, n):
            for m4 in range(8):
                bank, bkey = (B[2 + m4 % 4], f"B{2 + m4 % 4}")
                b3 = v4(bank[:])
                for c in range(4):
                    m = m4 * 4 + c
                    for k in range(8):
                        mm(b3[:, c, 0:n], WUP[:, k, m * 128:(m + 1) * 128], xT[:, k, 0:n], k == 0, k == 7,
                           [tk, f"w_up{m // 4}"], [bkey])
                act(v4(rT)[:, :, 0:n], b3[:, :, 0:n], AF.Relu, [bkey], ["rT"])
                ve('pool', 'tensor_tensor', ["rT"], ["aT"], aT[:, m4 * 4:(m4 + 1) * 4, 0:n], v4(rT)[:, :, 0:n], v4(rT)[:, :, 0:n], ALU.mult)

        def ffn_down(x, xk, n, out_rows):
            for nb, (bank, bkey) in enumerate(((B[0], "B0"), (B[1], "B1"))):
                for k in range(32):
                    mm(bank[0:n, :], aT[:, k, 0:n], WDN[:, k, nb * 512:(nb + 1) * 512], k == 0, k == 31,
                       ["aT", f"w_dn{k // 4}"], [bkey])
                ve('dve', 'tensor_tensor', [bkey, xk], [xk], x[0:n, nb * 512:(nb + 1) * 512], bank[0:n, :],
                   x[0:n, nb * 512:(nb + 1) * 512], ALU.add)
            rstd_rows(x, xk, n, junk=(mixT, "mixT"))
            S.add('dve', lambda e: e.scalar_tensor_tensor(x[0:n, :], x[0:n, :], ss[0:n, 2:3], gf[0:n, :],
                                                          ALU.mult, ALU.mult), [xk, "ss2", "gf"], [xk])
            dma('sp', out_rows, x[0:n, :], [xk], [])

        LB = Loader([(h1s[t * 128:(t + 1) * 128, :], 128, [f"h1s{t}"]) for t in range(nt)])
        tiles = []
        if do_sample:
            tiles.append(("s", None))
        tiles += [("m", t) for t in range(nt)]

        def tile_x(tl):
            if tl[0] == "s":
                return x_s, xk_s, NS, ys[:, :]
            xx, xxk = LB.get(tl[1], ahead=0)
            return xx, xxk, 128, y[tl[1] * 128:(tl[1] + 1) * 128, :]

        cur_t = tile_x(tiles[0]) if tiles else None
        cur_f = front(cur_t[0], cur_t[1], cur_t[2], 1) if tiles else None
        for i, tl in enumerate(tiles):
            x, xk, n, orow = cur_t
            xT, tk = cur_f
            ffn_up(xT, tk, n)
            if i + 1 < len(tiles):
                nxt_t = tile_x(tiles[i + 1])
                nxt_f = front(nxt_t[0], nxt_t[1], nxt_t[2], 1)
            ffn_down(x, xk, n, orow)
            if i + 1 < len(tiles):
                cur_t, cur_f = nxt_t, nxt_f

    return finish()


_CACHE = {}


def _prep_inputs(inp):
    f32 = np.float32
    xpmt = np.asarray(inp['x_prompt'], f32)
    xsm = np.asarray(inp['x_sample'], f32)
    meta = np.asarray(inp['meta'], f32)
    w_in = np.asarray(inp['w_in'], f32)[0]
    perm = np.array([(4 * h + g) * 64 + d for g in range(4) for h in range(2) for d in range(64)])
    w_in_p = np.ascontiguousarray(np.concatenate([w_in[:, perm], w_in[:, 512:]], axis=1))
    w_ao_p = np.ascontiguousarray(np.asarray(inp['w_att_out'], f32)[0][perm, :])
    w_ho = np.ascontiguousarray(np.asarray(inp['w_hg_out'], f32)[0])
    w_o = np.ascontiguousarray(np.asarray(inp['w_o'], f32)[0])
    w_up = np.ascontiguousarray(np.asarray(inp['w_up'], f32)[0])
    w_dn = np.ascontiguousarray(np.asarray(inp['w_down'], f32)[0])
    g3 = np.stack([np.asarray(inp['ln_mix'], f32)[0], np.asarray(inp['ln_ffn'], f32)[0],
                   np.asarray(inp['ln_f'], f32)])
    gains = np.ascontiguousarray(g3.reshape(3, 8, 128).transpose(2, 0, 1))
    gains_f = np.ascontiguousarray(np.broadcast_to(g3[2][None, :], (128, D)))
    gains_m = np.ascontiguousarray(np.broadcast_to(g3[0][None, :], (128, D)))
    gains_b = np.ascontiguousarray(np.broadcast_to(g3[1][None, :], (128, D)))
    lb = np.asarray(inp['lb_param'], f32)
    lbp = np.ascontiguousarray(lb.reshape(2, 4, 128).transpose(2, 0, 1).reshape(128, 8))
    hgn = np.ascontiguousarray(np.asarray(inp['hg_norm'], f32)[0].reshape(128, 1))
    sk = np.asarray(inp['sinks'], f32)[0]
    snk = np.ascontiguousarray(np.repeat(sk.reshape(2, 1, 4), 64, axis=1).reshape(128, 4))
    ck = np.asarray(inp['cache_k'], f32)[0].reshape(128, 128, 128)
    cv = np.asarray(inp['cache_v'], f32)[0].reshape(128, 128, 128)
    cmk = np.asarray(inp['cache_meta_k'], f32)[0].reshape(128, 16, 128)
    cmv = np.asarray(inp['cache_meta_v'], f32)[0].reshape(128, 16, 128)
    sth = np.asarray(inp['state_hgrn'], f32)[0]
    maps = []
    for c in range(NCORES):
        b, j = c // NSEG, c % NSEG
        npad = NPRE * 128 - 16 - TOK * j
        xp = np.concatenate([np.zeros((npad, D), f32), meta, xpmt[b, 0:TOK * j]], axis=0)
        xp = xp.reshape(-1, 64, D)[:, ::-1, :].reshape(-1, D)
        xhalo = xpmt[b, TOK * j - 128:TOK * j] if j > 0 else np.zeros((128, D), f32)
        maps.append(dict(
            xp=np.ascontiguousarray(xp), xmeta=meta, xhalo=np.ascontiguousarray(xhalo),
            xmain=np.ascontiguousarray(xpmt[b, TOK * j:TOK * (j + 1)]),
            xs=np.ascontiguousarray(xsm[NS * c:NS * (c + 1), 0, :]),
            w_in=w_in_p, w_ao=w_ao_p, w_ho=w_ho, w_o=w_o, w_up=w_up, w_dn=w_dn,
            gains=gains, gains_f=gains_f, gains_m=gains_m, gains_b=gains_b, lbp=lbp, hgn=hgn, snk=snk,
            flag=np.full((128, 1), 1.0 if j > 0 else 0.0, f32),
            ck=np.ascontiguousarray(ck[NS * c:NS * (c + 1)]), cv=np.ascontiguousarray(cv[NS * c:NS * (c + 1)]),
            cmk=np.ascontiguousarray(cmk[NS * c:NS * (c + 1)]), cmv=np.ascontiguousarray(cmv[NS * c:NS * (c + 1)]),
            sth=np.ascontiguousarray(sth[NS * c:NS * (c + 1)]),
        ))
    return maps


def kernel(**inp):
    if 'nc' not in _CACHE:
        _CACHE['nc'] = build_program()[0]
    nc = _CACHE['nc']
    maps = _prep_inputs(inp)
    res = run_bass_kernel_spmd(nc, maps, core_ids=list(range(NCORES)))
    R = res.results
    f32 = np.float32
    y_prompt = np.zeros((2, SEQ, D), f32)
    y_sample = np.zeros((128, 1, D), f32)
    nk = np.zeros((1, 2, 128, 2, 64), f32)
    nv = np.zeros((1, 2, 128, 2, 64), f32)
    nmk = np.zeros((1, 2, 16, 2, 64), f32)
    nmv = np.zeros((1, 2, 16, 2, 64), f32)
    nsp = np.zeros((1, 2, 4, 128, 128), f32)
    nks = np.zeros((1, 128, 128, 2, 64), f32)
    nvs = np.zeros((1, 128, 128, 2, 64), f32)
    nss = np.zeros((1, 128, 4, 128, 128), f32)
    for c in range(NCORES):
        b, j = c // NSEG, c % NSEG
        r = R[c]
        y_prompt[b, TOK * j:TOK * (j + 1)] = r['y']
        y_sample[NS * c:NS * (c + 1), 0] = r['ys']
        if j == NSEG - 1:
            nk[0, b] = r['kp'].reshape(128, 2, 64)
            nv[0, b] = r['vp'].reshape(128, 2, 64)
            nsp[0, b] = r['spo']
        if j == 0:
            nmk[0, b] = r['mk'].reshape(16, 2, 64)
            nmv[0, b] = r['mv'].reshape(16, 2, 64)
        nks[0, NS * c:NS * (c + 1)] = r['ksn'].reshape(NS, 128, 2, 64)
        nvs[0, NS * c:NS * (c + 1)] = r['vsn'].reshape(NS, 128, 2, 64)
        nss[0, NS * c:NS * (c + 1)] = r['ssn']
    return (y_prompt, y_sample, nk, nv, nmk, nmv, nsp, nks, nvs, nss)
```
